# Optimizing a Trainium2 kernel written in Bass

```python
import jax, jax.numpy as jnp
from jax import lax
import numpy as np

D_MODEL = 2048
BATCH = 1
SEQ = 8192
DEPTH = 4

CHUNK = 64
N_A_LAYERS = DEPTH // 2
N_B_LAYERS = DEPTH - N_A_LAYERS
POOL_WINDOWS = (2, 4, 8, 16)
N_POOL_GROUPS = len(POOL_WINDOWS)
POOL_GROUP = D_MODEL // N_POOL_GROUPS
N_HEADS = 16
HEAD_DIM = D_MODEL // N_HEADS
LEFT_CHUNKS = 8
LEFT = LEFT_CHUNKS * CHUNK
BAND = (LEFT_CHUNKS + 1) * CHUNK
REL_MAX = 128
N_REL = (CHUNK - 1) + REL_MAX + 1
D_FF = 5504
EPS = 1e-6
NEG_INF = -1e30

kernel_name = "yoco_pool_chunked_relbias_macaron"


def rmsnorm(x, g):
    xf = x.astype(jnp.float32)
    y = xf * lax.rsqrt(jnp.mean(xf * xf, axis=-1, keepdims=True) + EPS)
    return (y * g.astype(jnp.float32)).astype(x.dtype)


def swiglu(h, w_gate, w_up, w_down):
    return (jax.nn.silu(h @ w_gate) * (h @ w_up)) @ w_down


def pool_mixer(h, w_pool, scale):
    B, S, D = h.shape
    hf = h.astype(jnp.float32).reshape(B, S, N_POOL_GROUPS, POOL_GROUP)
    cs = jnp.cumsum(hf, axis=1)
    t = jnp.arange(S)
    pooled = []
    for g, w in enumerate(POOL_WINDOWS):
        csg = cs[:, :, g]
        prev = jnp.pad(csg, ((0, 0), (w, 0), (0, 0)))[:, :S]
        cnt = jnp.minimum(t + 1, w).astype(jnp.float32)[None, :, None]
        pooled.append((csg - prev) / cnt)
    pooled = jnp.stack(pooled, axis=2)
    diff = (pooled - hf).astype(h.dtype)
    y = jnp.einsum('bsgc,gcd->bsgd', diff, w_pool).reshape(B, S, D)
    return y * scale


def head_rmsnorm(t, g):
    tf = t.astype(jnp.float32)
    y = tf * lax.rsqrt(jnp.mean(tf * tf, axis=-1, keepdims=True) + EPS)
    return (y * g.astype(jnp.float32)).astype(t.dtype)


def shared_kv(x, kv_norm, w_k, w_v, k_gain):
    B, S, _ = x.shape
    hk = rmsnorm(x, kv_norm)
    k = head_rmsnorm((hk @ w_k).reshape(B, S, N_HEADS, HEAD_DIM), k_gain)
    v = (hk @ w_v).reshape(B, S, N_HEADS, HEAD_DIM)
    pad = ((0, 0), (LEFT, 0), (0, 0), (0, 0))
    return jnp.pad(k, pad), jnp.pad(v, pad)


def chunked_attention(h, w_q, q_gain, rel_table, w_o, k_pad, v_pad):
    B, S, D = h.shape
    nc = S // CHUNK
    q = head_rmsnorm((h @ w_q).reshape(B, S, N_HEADS, HEAD_DIM), q_gain)
    qc = q.reshape(B, nc, CHUNK, N_HEADS, HEAD_DIM).transpose(1, 0, 2, 3, 4)
    r = jnp.arange(CHUNK)[:, None]
    m = jnp.arange(BAND)[None, :]
    rel = r - m + LEFT
    idx = jnp.clip(rel, -(CHUNK - 1), REL_MAX) + (CHUNK - 1)
    bias = rel_table.astype(jnp.float32)[:, idx]
    scale = HEAD_DIM ** -0.5
    key_off = jnp.arange(BAND)

    def one_chunk(args):
        q_blk, c = args
        start = c * CHUNK
        kb = lax.dynamic_slice_in_dim(k_pad, start, BAND, axis=1)
        vb = lax.dynamic_slice_in_dim(v_pad, start, BAND, axis=1)
        s = jnp.einsum('bqhd,bkhd->bhqk', q_blk.astype(jnp.float32), kb.astype(jnp.float32)) * scale
        s = s + bias[None]
        valid = (start + key_off) >= LEFT
        s = jnp.where(valid[None, None, None, :], s, NEG_INF)
        p = jax.nn.softmax(s, axis=-1)
        o = jnp.einsum('bhqk,bkhd->bqhd', p, vb.astype(jnp.float32))
        return o.astype(h.dtype)

    out = lax.map(one_chunk, (qc, jnp.arange(nc)))
    out = out.transpose(1, 0, 2, 3, 4).reshape(B, S, D)
    return out @ w_o


def setup_inputs(seed: int = 0) -> dict:
    key = jax.random.key(seed)
    ks = jax.random.split(key, 24)
    f32 = jnp.float32

    def nrm(k, shape, s):
        return jax.random.normal(k, shape, f32) * s

    def gain(k, shape):
        return jnp.ones(shape, f32) + 0.05 * jax.random.normal(k, shape, f32)

    return {
        "x": jax.random.normal(ks[0], (BATCH, SEQ, D_MODEL), f32),
        "ffn1_norm": gain(ks[1], (DEPTH, D_MODEL)),
        "ffn1_w_gate": nrm(ks[2], (DEPTH, D_MODEL, D_FF), D_MODEL ** -0.5),
        "ffn1_w_up": nrm(ks[3], (DEPTH, D_MODEL, D_FF), D_MODEL ** -0.5),
        "ffn1_w_down": nrm(ks[4], (DEPTH, D_FF, D_MODEL), D_FF ** -0.5),
        "mix_norm": gain(ks[5], (DEPTH, D_MODEL)),
        "ffn2_norm": gain(ks[6], (DEPTH, D_MODEL)),
        "ffn2_w_gate": nrm(ks[7], (DEPTH, D_MODEL, D_FF), D_MODEL ** -0.5),
        "ffn2_w_up": nrm(ks[8], (DEPTH, D_MODEL, D_FF), D_MODEL ** -0.5),
        "ffn2_w_down": nrm(ks[9], (DEPTH, D_FF, D_MODEL), D_FF ** -0.5),
        "pool_w": nrm(ks[10], (N_A_LAYERS, N_POOL_GROUPS, POOL_GROUP, POOL_GROUP), POOL_GROUP ** -0.5),
        "pool_scale": gain(ks[11], (N_A_LAYERS, D_MODEL)),
        "kv_norm": gain(ks[12], (D_MODEL,)),
        "w_k": nrm(ks[13], (D_MODEL, D_MODEL), D_MODEL ** -0.5),
        "w_v": nrm(ks[14], (D_MODEL, D_MODEL), D_MODEL ** -0.5),
        "k_gain": gain(ks[15], (HEAD_DIM,)),
        "w_q": nrm(ks[16], (N_B_LAYERS, D_MODEL, D_MODEL), D_MODEL ** -0.5),
        "q_gain": gain(ks[17], (N_B_LAYERS, HEAD_DIM)),
        "rel_bias": nrm(ks[18], (N_B_LAYERS, N_HEADS, N_REL), 0.5),
        "w_o": nrm(ks[19], (N_B_LAYERS, D_MODEL, D_MODEL), D_MODEL ** -0.5),
    }


def reference(x, ffn1_norm, ffn1_w_gate, ffn1_w_up, ffn1_w_down, mix_norm, ffn2_norm,
              ffn2_w_gate, ffn2_w_up, ffn2_w_down, pool_w, pool_scale, kv_norm, w_k, w_v,
              k_gain, w_q, q_gain, rel_bias, w_o):
    k_pad, v_pad = None, None
    for l in range(DEPTH):
        x = x + 0.5 * swiglu(rmsnorm(x, ffn1_norm[l]), ffn1_w_gate[l], ffn1_w_up[l], ffn1_w_down[l])
        h = rmsnorm(x, mix_norm[l])
        if l < N_A_LAYERS:
            x = x + pool_mixer(h, pool_w[l], pool_scale[l])
        else:
            b = l - N_A_LAYERS
            x = x + chunked_attention(h, w_q[b], q_gain[b], rel_bias[b], w_o[b], k_pad, v_pad)
        x = x + 0.5 * swiglu(rmsnorm(x, ffn2_norm[l]), ffn2_w_gate[l], ffn2_w_up[l], ffn2_w_down[l])
        if l == N_A_LAYERS - 1:
            k_pad, v_pad = shared_kv(x, kv_norm, w_k, w_v, k_gain)
    return x
```

```python
import numpy as np
import concourse.bass as bass
import concourse.mybir as mybir
from concourse.bass_utils import run_bass_kernel_spmd

F32 = mybir.dt.float32
BF16 = mybir.dt.bfloat16
AF = mybir.ActivationFunctionType
ALU = mybir.AluOpType

NCORES = 8
D = 2048
KC = 16
DFF = 5504
NFC = 43
NH = 16
DH = 128
SEQ = 8192
T_OWN = 1024
HALO_KV = 512
T_HALO = 544
NKEY = T_OWN + HALO_KV
EPS = 1e-6
NEG = -30000.0
POOL_W = (2, 4, 8, 16)
NS = 3
FG = 2

V_FFN1 = 0
V_MIX = 4
V_FFN2 = 8
V_KV = 12
V_PSC = 13
NVEC = 15

ENGS = ("pe", "act", "dve", "pool", "sp")


class Op:
    __slots__ = ("eng", "fn", "deps", "idx", "dkey", "val", "inc")


class Buf:
    __slots__ = ("w", "r", "off", "size", "name")

    def __init__(self, name=""):
        self.w = {}
        self.r = {}
        self.off = None
        self.size = 0
        self.name = name


class Prog:
    def __init__(self):
        self.ops = {e: [] for e in ENGS}
        self.live = []
        self.final = []

    def sbuf(self, off, size, name=""):
        b = Buf(name)
        b.off, b.size = off, size
        keep = []
        for o in self.live:
            if o.off < off + size and off < o.off + o.size:
                for k, op in o.w.items():
                    b.r[("a", id(op))] = op
                for k, op in o.r.items():
                    b.r[("a", id(op))] = op
            else:
                keep.append(o)
        keep.append(b)
        self.live = keep
        return b

    def emit(self, eng, fn, reads=(), writes=(), pwrites=(), dkey=None):
        op = Op()
        op.eng, op.fn, op.dkey = eng, fn, dkey
        op.inc = dkey is not None
        op.val = 0
        op.idx = len(self.ops[eng])
        deps = {}
        for b in reads:
            for o in b.w.values():
                deps[id(o)] = o
        for b in writes:
            for o in b.w.values():
                deps[id(o)] = o
            for o in b.r.values():
                deps[id(o)] = o
        for b in pwrites:
            for o in b.r.values():
                deps[id(o)] = o
        op.deps = list(deps.values())
        key = dkey if dkey else eng
        for b in reads:
            if b not in writes:
                b.r[key] = op
        for b in writes:
            b.w = {key: op}
            b.r = {}
        for b in pwrites:
            b.w[key] = op
        self.ops[eng].append(op)
        return op

    @staticmethod
    def _needs_wait(op, d):
        if d.dkey is None and op.dkey is None and d.eng == op.eng:
            if op.eng == "pe":
                return False
            return (op.idx - d.idx) <= 2
        return True

    def lower(self, nc):
        for e in ENGS:
            for op in self.ops[e]:
                for d in op.deps:
                    if self._needs_wait(op, d):
                        d.inc = True
        dcount = {}
        for e in ENGS:
            c = 0
            for op in self.ops[e]:
                if op.dkey is None:
                    if op.inc:
                        c += 1
                    op.val = c
                else:
                    dcount[op.dkey] = dcount.get(op.dkey, 0) + 16
                    op.val = dcount[op.dkey]
        esem = {e: nc.alloc_semaphore("s_" + e) for e in ENGS}
        dsem = {k: nc.alloc_semaphore("d_" + k) for k in dcount}
        finals = self.final

        def run(e, eng):
            waited = {}
            for op in self.ops[e]:
                for d in op.deps:
                    if not self._needs_wait(op, d):
                        continue
                    key = ("d", d.dkey) if d.dkey else ("e", d.eng)
                    if waited.get(key, 0) >= d.val:
                        continue
                    sem = dsem[d.dkey] if d.dkey else esem[d.eng]
                    eng.wait_ge(sem, d.val)
                    waited[key] = d.val
                ins = op.fn(eng)
                if op.dkey:
                    ins.then_inc(dsem[op.dkey], 16)
                elif op.inc:
                    ins.then_inc(esem[e], 1)
            if e == "sp":
                for op in finals:
                    key = ("d", op.dkey)
                    if waited.get(key, 0) < op.val:
                        eng.wait_ge(dsem[op.dkey], op.val)
                        waited[key] = op.val

        with nc.Block() as block:
            @block.tensor
            def _(eng):
                run("pe", eng)

            @block.scalar
            def _(eng):
                run("act", eng)

            @block.vector
            def _(eng):
                run("dve", eng)

            @block.gpsimd
            def _(eng):
                run("pool", eng)

            @block.sync
            def _(eng):
                run("sp", eng)


def build(plan, kv_kind="Internal"):
    nc = bass.Bass("TRN2", target_bir_lowering=False)
    pr = Prog()
    SHAPES = {"xT": [D, T_OWN], "xhT": [D, T_HALO], "vecs": [128, NVEC * KC], "hg": [128, 4],
              "hmask": [128, 1], "invc": [128, 64], "maskG": [128, 640], "biasG": [2, NH, 128, 640],
              "ffn1_w_gate": [4, D, DFF], "ffn1_w_up": [4, D, DFF], "ffn2_w_gate": [4, D, DFF],
              "ffn2_w_up": [4, D, DFF], "ffn1_w_down": [4, DFF, D], "ffn2_w_down": [4, DFF, D],
              "pool_w": [2, 4, 512, 512], "w_k": [D, D], "w_v": [D, D], "w_q": [2, D, D], "w_o": [2, D, D]}

    class _Lazy(dict):
        def __missing__(self, name):
            self[name] = nc.dram_tensor(name, list(SHAPES[name]), F32, kind="ExternalInput").ap()
            return self[name]
    din = _Lazy()
    outT = nc.dram_tensor("outT", [D, T_OWN], F32, kind="ExternalOutput").ap()
    kvk = kv_kind
    KTd = nc.dram_tensor("KTd", [NH, 128, NKEY], BF16, kind=kvk).ap()
    Vd = nc.dram_tensor("Vd", [NKEY, D], BF16, kind=kvk).ap()
    KTd_b = [Buf("KTd%d" % h) for h in range(NH)]
    Vd_b = Buf("Vd")

    TOT = 212000
    big = nc.alloc_sbuf_tensor("big", [128, TOT // 4], F32)

    def view(off, n, dt):
        assert off % 4 == 0
        if dt == F32:
            return big[:, off // 4: off // 4 + n]
        nb = n * 2
        assert nb % 4 == 0
        return big[:, off // 4: off // 4 + nb // 4].bitcast(BF16)

    OFF_X = 0
    OFF_H = 65536
    OFF_C = 98304
    o = OFF_C
    OFF_VEC = o; o += NVEC * KC * 4
    OFF_HG = o; o += 16
    OFF_HMASK = o; o += 4
    OFF_ZCOL = o; o += 4
    OFF_EPSC = o; o += 4
    OFF_EPSQ = o; o += 4
    OFF_INVC = o; o += 256
    OFF_MASKG = o; o += 2560
    OFF_ONES = o; o += 256
    OFF_PHALO = o; o += 2 * KC * 16 * 4
    OFF_ZERO16 = o; o += 64
    OFF_RSTD = o; o += 4096
    OFF_SQ = o; o += 3 * 1024
    OFF_WG = o; o += NS * 8192
    OFF_R = o
    assert OFF_R + 61440 <= TOT, OFF_R

    VEC = view(OFF_VEC, NVEC * KC, F32)
    HG = view(OFF_HG, 4, F32)
    HMASK = view(OFF_HMASK, 1, F32)
    ZCOL = view(OFF_ZCOL, 1, F32)
    EPSC = view(OFF_EPSC, 1, F32)
    EPSQ = view(OFF_EPSQ, 1, F32)
    INVC = view(OFF_INVC, 64, F32)
    MASKG = view(OFF_MASKG, 640, F32)
    ONES = view(OFF_ONES, 128, BF16)
    PHALO = view(OFF_PHALO, 2 * KC * 16, F32)
    ZERO16 = view(OFF_ZERO16, 16, F32)
    RSTD = view(OFF_RSTD, 1024, F32)
    const_b = pr.sbuf(OFF_VEC, OFF_PHALO - OFF_VEC, "const")
    phalo_b = [[pr.sbuf(OFF_PHALO + (l * KC + kc) * 64, 64, "ph") for kc in range(KC)] for l in range(2)]
    zero16_b = pr.sbuf(OFF_ZERO16, 64, "z16")
    SQ = [view(OFF_SQ + i * 1024, 512, BF16) for i in range(3)]
    sq_b = [pr.sbuf(OFF_SQ + i * 1024, 1024, "sq") for i in range(3)]
    wg_b = [pr.sbuf(OFF_WG + i * 8192, 8192, "wg%d" % i) for i in range(NS)]

    def wgv(i):
        return view(OFF_WG + i * 8192, 4096, BF16)

    PS = [nc.alloc_psum_tensor("ps%d" % i, [128, 512], F32) for i in range(8)]
    ps_b = [Buf("ps%d" % i) for i in range(8)]

    st = dict(sq=0, wg=0, gen=0, ss=0, T=None, tiles=None, X=None, H=None, rstd=None)

    def Xap(kc, c0, w):
        return view(OFF_X + (kc * 1024 + c0) * 4, w, F32)

    def Hap(kc, c0, w):
        return view(OFF_H + (kc * 1024 + c0) * 2, w, BF16)

    def mk_xbufs(tiles):
        return [[pr.sbuf(OFF_X + (kc * 1024 + c0) * 4, w * 4, "x") for (c0, w) in tiles] for kc in range(KC)]

    def set_pass(T, tiles, xbufs=None):
        st["T"], st["tiles"] = T, tiles
        st["X"] = xbufs if xbufs is not None else mk_xbufs(tiles)
        st["H"] = [[pr.sbuf(OFF_H + (kc * 1024 + c0) * 2, w * 2, "h") for (c0, w) in tiles] for kc in range(KC)]
        st["rstd"] = [pr.sbuf(OFF_RSTD + c0 * 4, w * 4, "rstd") for (c0, w) in tiles]

    def tiles_overlapping(c0, w):
        return [i for i, (a, b) in enumerate(st["tiles"]) if a < c0 + w and c0 < a + b]

    def setup():
        pr.emit("sp", lambda e: e.dma_start(out=VEC, in_=din["vecs"]), writes=[const_b], dkey="c0")
        pr.emit("sp", lambda e: e.dma_start(out=HG, in_=din["hg"]), pwrites=[const_b], dkey="c1")
        pr.emit("sp", lambda e: e.dma_start(out=HMASK, in_=din["hmask"]), pwrites=[const_b], dkey="c2")
        pr.emit("sp", lambda e: e.dma_start(out=INVC, in_=din["invc"]), pwrites=[const_b], dkey="c3")
        pr.emit("sp", lambda e: e.dma_start(out=MASKG, in_=din["maskG"]), pwrites=[const_b], dkey="c4")
        pr.emit("dve", lambda e: e.memset(ONES, 1.0), pwrites=[const_b])
        pr.emit("dve", lambda e: e.memset(ZCOL, 0.0), pwrites=[const_b])
        pr.emit("dve", lambda e: e.memset(EPSC, EPS), pwrites=[const_b])
        pr.emit("dve", lambda e: e.memset(EPSQ, EPS * DH), pwrites=[const_b])
        pr.emit("dve", lambda e: e.memset(ZERO16, 0.0), writes=[zero16_b])

    def prefetch_x(src, T, tiles):
        xb = mk_xbufs(tiles)
        s3 = src.rearrange("(kc p) t -> p kc t", p=128)
        for tt, (c0, w) in enumerate(tiles):
            for q in range(4):
                dst = view(OFF_X + q * 4 * 4096, 4 * 1024, F32).rearrange("p (k t) -> p k t", k=4)[:, :, c0:c0 + w]
                bl = [xb[kc][tt] for kc in range(q * 4, q * 4 + 4)]
                pr.emit("sp", lambda e, dst=dst, q=q, c0=c0, w=w: e.dma_start(
                    out=dst, in_=s3[:, q * 4:(q + 1) * 4, c0:c0 + w]),
                    writes=bl, dkey="xin%d" % (tt * 4 + q))
        return xb

    def load_x(src, T, tiles):
        xb = st.pop("xpre", None)
        if xb is None:
            xb = prefetch_x(src, T, tiles)
        set_pass(T, tiles, xb)

    def store_x():
        T, tiles = st["T"], st["tiles"]
        d3 = outT.rearrange("(kc p) t -> p kc t", p=128)
        for tt, (c0, w) in enumerate(tiles):
            for q in range(4):
                srcv = view(OFF_X + q * 4 * 4096, 4 * 1024, F32).rearrange("p (k t) -> p k t", k=4)[:, :, c0:c0 + w]
                bl = [st["X"][kc][tt] for kc in range(q * 4, q * 4 + 4)]
                op = pr.emit("sp", lambda e, srcv=srcv, q=q, c0=c0, w=w: e.dma_start(
                    out=d3[:, q * 4:(q + 1) * 4, c0:c0 + w], in_=srcv),
                    reads=bl, dkey="xout%d" % (tt * 4 + q))
                pr.final.append(op)

    def rmsnorm(vidx, write_h=True):
        X, H = st["X"], st["H"]
        for tt, (c0, w) in enumerate(st["tiles"]):
            ssi = st["ss"] % 2
            st["ss"] += 1
            ssb, ssB = PS[ssi], ps_b[ssi]
            for kc in range(KC):
                si = st["sq"] % 3
                st["sq"] += 1
                pr.emit("act", lambda e, si=si, kc=kc, c0=c0, w=w: e.activation(
                    out=SQ[si][:, 0:w], in_=Xap(kc, c0, w), func=AF.Square),
                    reads=[X[kc][tt]], writes=[sq_b[si]])
                pr.emit("pe", lambda e, si=si, kc=kc, w=w, ssb=ssb: e.matmul(
                    ssb[:, 0:w], ONES, SQ[si][:, 0:w], start=(kc == 0), stop=(kc == KC - 1)),
                    reads=[sq_b[si], const_b], writes=[ssB])
            rs = RSTD[:, c0:c0 + w]
            pr.emit("act", lambda e, rs=rs, ssb=ssb, w=w: e.activation(
                out=rs, in_=ssb[:, 0:w], func=AF.Ln, bias=EPSC[:, 0:1], scale=1.0 / D),
                reads=[ssB, const_b], writes=[st["rstd"][tt]])
            pr.emit("act", lambda e, rs=rs: e.activation(out=rs, in_=rs, func=AF.Exp, scale=-0.5),
                    reads=[st["rstd"][tt]], writes=[st["rstd"][tt]])
            for kc in range(KC if write_h else 0):
                col = vidx * KC + kc
                pr.emit("dve", lambda e, kc=kc, c0=c0, w=w, col=col, rs=rs: e.scalar_tensor_tensor(
                    out=Hap(kc, c0, w), in0=Xap(kc, c0, w), scalar=VEC[:, col:col + 1], in1=rs,
                    op0=ALU.mult, op1=ALU.mult),
                    reads=[X[kc][tt], st["rstd"][tt], const_b], writes=[H[kc][tt]])

    def load_wg_slot(src3, ncols_total, shape3):
        i = st["wg"] % NS
        st["wg"] += 1
        a, b = shape3
        dst = wgv(i)[:, 0:a * b].rearrange("p (a b) -> p a b", a=a)
        pr.emit("pool", lambda e, dst=dst, src3=src3: e.dma_start(out=dst, in_=src3),
                writes=[wg_b[i]], dkey="wg%d" % i)
        return i

    def ffn(l, which):
        X, H, tiles = st["X"], st["H"], st["tiles"]
        nt = len(tiles)
        pre = "ffn%d_" % which
        wg3 = din[pre + "w_gate"][l].rearrange("(kc p) f -> p kc f", p=128)
        wu3 = din[pre + "w_up"][l].rearrange("(kc p) f -> p kc f", p=128)
        wd3 = din[pre + "w_down"][l].rearrange("(fc p) d -> p fc d", p=128)
        rmsnorm((V_FFN1 if which == 1 else V_FFN2) + l)
        OFF_WU = OFF_R
        OFF_WD = OFF_WU + NS * 8192
        OFF_ACT = OFF_WD + NS * 8192
        OFF_SG = OFF_ACT + 2 * 4096
        wu_b = [pr.sbuf(OFF_WU + i * 8192, 8192, "wu") for i in range(NS)]
        wd_b = [pr.sbuf(OFF_WD + i * 8192, 8192, "wd") for i in range(NS)]
        act_b = [[[pr.sbuf(OFF_ACT + s * 4096 + (j * 1024 + c0) * 2, w * 2, "act") for (c0, w) in tiles]
                  for j in range(FG)] for s in range(2)]
        sg_b = [pr.sbuf(OFF_SG + i * 2048, 2048, "sg") for i in range(2)]

        def WU3(i):
            return view(OFF_WU + i * 8192, 4096, BF16).rearrange("p (a b) -> p a b", a=KC)

        def WG3(i):
            return wgv(i).rearrange("p (a b) -> p a b", a=KC)

        def WD3(i):
            return view(OFF_WD + i * 8192, 4096, BF16).rearrange("p (a b) -> p a b", a=FG)

        def ACTap(s, j, c0, w):
            return view(OFF_ACT + s * 4096 + (j * 1024 + c0) * 2, w, BF16)

        def SGap(i, w):
            return view(OFF_SG + i * 2048, w, F32)

        groups = []
        f = 0
        while f < NFC:
            n = min(FG, NFC - f)
            groups.append((f, n))
            f += n
        slots = {}
        cnt = dict(gu=0, d=0, ring=0)

        def load_group(g):
            f0, n = groups[g]
            gi = st["wg"] % NS
            st["wg"] += 1
            ri = cnt["ring"] % NS
            cnt["ring"] += 1
            fw = n * 128
            pr.emit("pool", lambda e, gi=gi, f0=f0, fw=fw: e.dma_start(
                out=WG3(gi)[:, :, 0:fw], in_=wg3[:, :, f0 * 128:f0 * 128 + fw]),
                writes=[wg_b[gi]], dkey="wg%d" % gi)
            pr.emit("pool", lambda e, ri=ri, f0=f0, fw=fw: e.dma_start(
                out=WU3(ri)[:, :, 0:fw], in_=wu3[:, :, f0 * 128:f0 * 128 + fw]),
                writes=[wu_b[ri]], dkey="wu%d" % ri)
            pr.emit("pool", lambda e, ri=ri, f0=f0, n=n: e.dma_start(
                out=WD3(ri)[:, 0:n, :], in_=wd3[:, f0:f0 + n, :]),
                writes=[wd_b[ri]], dkey="wd%d" % ri)
            slots[g] = (gi, ri)

        def gu_tile(g, j, tt, dl=()):
            gi, ri = slots[g]
            c0, w = tiles[tt]
            k = cnt["gu"]
            cnt["gu"] += 1
            gb, gB = PS[k % 2], ps_b[k % 2]
            ub, uB = PS[2 + k % 2], ps_b[2 + k % 2]
            dl = list(dl)
            nseg = 4
            per = (len(dl) + nseg - 1) // nseg
            segs = [dl[i * per:(i + 1) * per] for i in range(nseg)]

            def gate(kc):
                pr.emit("pe", lambda e, kc=kc: e.matmul(
                    gb[:, 0:w], WG3(gi)[:, kc, j * 128:(j + 1) * 128], Hap(kc, c0, w),
                    start=(kc == 0), stop=(kc == KC - 1)),
                    reads=[wg_b[gi], H[kc][tt]], writes=[gB])

            def up(kc):
                pr.emit("pe", lambda e, kc=kc: e.matmul(
                    ub[:, 0:w], WU3(ri)[:, kc, j * 128:(j + 1) * 128], Hap(kc, c0, w),
                    start=(kc == 0), stop=(kc == KC - 1)),
                    reads=[wu_b[ri], H[kc][tt]], writes=[uB])
            for kc in range(0, 8):
                gate(kc)
            for a in segs[0]:
                d_tile(*a)
            for kc in range(8, KC):
                gate(kc)
            pr.emit("act", lambda e: e.activation(
                out=SGap(k % 2, w), in_=gb[:, 0:w], func=AF.Silu),
                reads=[gB], writes=[sg_b[k % 2]])
            for a in segs[1]:
                d_tile(*a)
            for kc in range(0, 8):
                up(kc)
            for a in segs[2]:
                d_tile(*a)
            for kc in range(8, KC):
                up(kc)
            pr.emit("dve", lambda e: e.tensor_tensor(
                out=ACTap(g % 2, j, c0, w), in0=SGap(k % 2, w), in1=ub[:, 0:w], op=ALU.mult),
                reads=[sg_b[k % 2], uB], writes=[act_b[g % 2][j][tt]])
            for a in segs[3]:
                d_tile(*a)

        def d_tile(g, dc, tt):
            gi, ri = slots[g]
            f0, n = groups[g]
            c0, w = tiles[tt]
            k = cnt["d"]
            cnt["d"] += 1
            db, dB = PS[4 + k % 4], ps_b[4 + k % 4]
            for j in range(n):
                pr.emit("pe", lambda e, j=j, db=db, ri=ri, dc=dc, g=g, c0=c0, w=w, n=n: e.matmul(
                    db[:, 0:w], WD3(ri)[:, j, dc * 128:(dc + 1) * 128], ACTap(g % 2, j, c0, w),
                    start=(j == 0), stop=(j == n - 1)),
                    reads=[wd_b[ri], act_b[g % 2][j][tt]], writes=[dB])
            pr.emit("dve", lambda e, db=db, dc=dc, c0=c0, w=w: e.scalar_tensor_tensor(
                out=Xap(dc, c0, w), in0=db[:, 0:w], scalar=0.5, in1=Xap(dc, c0, w),
                op0=ALU.mult, op1=ALU.add),
                reads=[dB, X[dc][tt]], writes=[X[dc][tt]])

        ng = len(groups)
        for g in range(ng + 1):
            gu_list = []
            d_list = []
            if g < ng:
                load_group(g)
                gu_list = [(g, j, tt) for j in range(groups[g][1]) for tt in range(nt)]
            if g >= 1:
                d_list = [(g - 1, dc, tt) for tt in range(nt) for dc in range(KC)]
            if not gu_list:
                for a in d_list:
                    d_tile(*a)
                continue
            per = (len(d_list) + len(gu_list) - 1) // len(gu_list) if d_list else 0
            for ii, a in enumerate(gu_list):
                gu_tile(*a, dl=d_list[ii * per:(ii + 1) * per])

    def pool(l, halo_mode):
        X, H, tiles, T = st["X"], st["H"], st["tiles"], st["T"]
        nt = len(tiles)
        rmsnorm(V_MIX + l, write_h=False)
        LW = (16 + 1024) * 4
        OFF_HC = OFF_R
        OFF_SA = OFF_HC + 2 * LW
        OFF_T16 = OFF_SA + 2 * LW
        hc_b = [pr.sbuf(OFF_HC + i * LW, LW, "hc") for i in range(2)]
        sa_b = [pr.sbuf(OFF_SA + i * LW, LW, "sa") for i in range(2)]
        t16_b = pr.sbuf(OFF_T16, 64, "t16")
        T16 = view(OFF_T16, 16, F32)

        def HC(i):
            return view(OFF_HC + i * LW, 16 + 1024, F32)

        def SA(i):
            return view(OFF_SA + i * LW, 16 + 1024, F32)

        W = 16 + T
        for dc in range(KC):
            g = dc // 4
            win = POOL_W[g]
            hi = dc % 2
            hc, hcB = HC(hi), hc_b[hi]
            col = (V_MIX + l) * KC + dc
            xall = view(OFF_X + dc * 4096, T, F32)
            pr.emit("dve", lambda e, hc=hc, xall=xall, col=col, T=T: e.scalar_tensor_tensor(
                out=hc[:, 16:16 + T], in0=xall, scalar=VEC[:, col:col + 1], in1=RSTD[:, 0:T],
                op0=ALU.mult, op1=ALU.mult),
                reads=[X[dc][i] for i in range(nt)] + st["rstd"] + [const_b], pwrites=[hcB])
            if halo_mode:
                pr.emit("act", lambda e, hc=hc: e.activation(out=hc[:, 0:16], in_=ZERO16, func=AF.Copy),
                        reads=[zero16_b], pwrites=[hcB])
            else:
                ph = PHALO[:, (l * KC + dc) * 16:(l * KC + dc + 1) * 16]
                pr.emit("act", lambda e, hc=hc, ph=ph: e.activation(out=hc[:, 0:16], in_=ph, func=AF.Copy),
                        reads=[phalo_b[l][dc]], pwrites=[hcB])
            src, srcB = hc, hcB
            k = 1
            si = 0
            v0 = 0
            while k < win:
                dst, dstB = SA(si), sa_b[si]
                v1 = v0 + k
                pr.emit("dve", lambda e, dst=dst, src=src, k=k, W=W, v1=v1: e.tensor_tensor(
                    out=dst[:, v1:W], in0=src[:, v1:W], in1=src[:, v1 - k:W - k], op=ALU.add),
                    reads=[srcB], writes=[dstB])
                src, srcB = dst, dstB
                si ^= 1
                v0 = v1
                k *= 2
            pr.emit("dve", lambda e, src=src, hc=hc, dc=dc, T=T, win=win: e.scalar_tensor_tensor(
                out=Hap(dc, 0, T), in0=src[:, 16:16 + T], scalar=1.0 / win, in1=hc[:, 16:16 + T],
                op0=ALU.mult, op1=ALU.subtract),
                reads=[srcB, hcB], writes=[H[dc][i] for i in range(nt)])
            pr.emit("dve", lambda e, src=src, g=g: e.tensor_tensor(
                out=T16, in0=src[:, 16:32], in1=INVC[:, g * 16:(g + 1) * 16], op=ALU.mult),
                reads=[srcB, const_b], writes=[t16_b])
            pr.emit("dve", lambda e, hc=hc, dc=dc: e.tensor_tensor(
                out=Hap(dc, 0, 16), in0=T16, in1=hc[:, 16:32], op=ALU.subtract),
                reads=[t16_b, hcB], pwrites=[H[dc][0]])
            if halo_mode:
                ph = PHALO[:, (l * KC + dc) * 16:(l * KC + dc + 1) * 16]
                pr.emit("act", lambda e, hc=hc, ph=ph, T=T: e.activation(out=ph, in_=hc[:, T:T + 16], func=AF.Copy),
                        reads=[hcB], writes=[phalo_b[l][dc]])
        for g in range(4):
            src3 = din["pool_w"][l, g].rearrange("(kc p) d -> p kc d", p=128)
            i = load_wg_slot(src3, 512, (4, 512))
            W3 = wgv(i)[:, 0:2048].rearrange("p (a b) -> p a b", a=4)
            for dco in range(4):
                dc = g * 4 + dco
                col = (V_PSC + l) * KC + dc
                for tt, (c0, w) in enumerate(tiles):
                    k = st["gen"]
                    st["gen"] += 1
                    pb, pB = PS[4 + k % 4], ps_b[4 + k % 4]
                    for kc in range(4):
                        pr.emit("pe", lambda e, pb=pb, W3=W3, kc=kc, dco=dco, g=g, c0=c0, w=w: e.matmul(
                            pb[:, 0:w], W3[:, kc, dco * 128:(dco + 1) * 128], Hap(g * 4 + kc, c0, w),
                            start=(kc == 0), stop=(kc == 3)),
                            reads=[wg_b[i], H[g * 4 + kc][tt]], writes=[pB])
                    pr.emit("dve", lambda e, pb=pb, dc=dc, c0=c0, w=w, col=col: e.scalar_tensor_tensor(
                        out=Xap(dc, c0, w), in0=pb[:, 0:w], scalar=VEC[:, col:col + 1], in1=Xap(dc, c0, w),
                        op0=ALU.mult, op1=ALU.add),
                        reads=[pB, X[dc][tt], const_b], writes=[X[dc][tt]])

    def head_norm(pb, pB, w, gcol, out_ap, out_bufs, post_scale, rs_ap, rs_b, after=None):
        si = st["sq"] % 3
        st["sq"] += 1
        pr.emit("act", lambda e: e.activation(out=SQ[si][:, 0:w], in_=pb[:, 0:w], func=AF.Square),
                reads=[pB], writes=[sq_b[si]])

        def tail():
            ssi = st["ss"] % 2
            st["ss"] += 1
            ssb, ssB = PS[ssi], ps_b[ssi]
            pr.emit("pe", lambda e: e.matmul(ssb[:, 0:w], ONES, SQ[si][:, 0:w], start=True, stop=True),
                    reads=[sq_b[si], const_b], writes=[ssB])
            pr.emit("act", lambda e: e.activation(
                out=rs_ap, in_=ssb[:, 0:w], func=AF.Ln, bias=post_scale[1][:, 0:1], scale=post_scale[0]),
                reads=[ssB, const_b], writes=[rs_b])
            pr.emit("act", lambda e: e.activation(out=rs_ap, in_=rs_ap, func=AF.Exp, scale=-0.5),
                    reads=[rs_b], writes=[rs_b])
            pr.emit("dve", lambda e: e.scalar_tensor_tensor(
                out=out_ap, in0=pb[:, 0:w], scalar=HG[:, gcol:gcol + 1], in1=rs_ap, op0=ALU.mult, op1=ALU.mult),
                reads=[pB, rs_b, const_b], **out_bufs)
            if after is not None:
                after()
        return tail

    def kv(tok_lo, dst_lo, after_norm=None):
        X, H, tiles, T = st["X"], st["H"], st["tiles"], st["T"]
        rmsnorm(V_KV)
        if after_norm is not None:
            after_norm()
        OFF_KST = OFF_R
        OFF_VST = OFF_KST + 2 * 1024
        OFF_RS = OFF_VST + 2 * 512
        kst_b = [pr.sbuf(OFF_KST + i * 1024, 1024, "kst") for i in range(2)]
        vst_b = [pr.sbuf(OFF_VST + i * 512, 512, "vst") for i in range(2)]
        rs_b = [pr.sbuf(OFF_RS + i * 2048, 2048, "rs") for i in range(2)]
        cnt = 0
        pend_tail = [None]
        wk3 = din["w_k"].rearrange("(kc p) d -> p kc d", p=128)
        wv3 = din["w_v"].rearrange("(kc p) d -> p kc d", p=128)
        for hp in range(8):
            i = load_wg_slot(wk3[:, :, hp * 256:(hp + 1) * 256], 256, (KC, 256))
            W3 = wgv(i).rearrange("p (a b) -> p a b", a=KC)
            for hh in range(2):
                head = hp * 2 + hh
                for tt, (c0, w) in enumerate(tiles):
                    k = st["gen"]
                    st["gen"] += 1
                    pb, pB = PS[4 + k % 4], ps_b[4 + k % 4]
                    for kc in range(KC):
                        pr.emit("pe", lambda e, pb=pb, W3=W3, kc=kc, hh=hh, c0=c0, w=w: e.matmul(
                            pb[:, 0:w], W3[:, kc, hh * 128:(hh + 1) * 128], Hap(kc, c0, w),
                            start=(kc == 0), stop=(kc == KC - 1)),
                            reads=[wg_b[i], H[kc][tt]], writes=[pB])
                    ki = cnt % 2
                    cnt += 1
                    kst = view(OFF_KST + ki * 1024, 512, BF16)
                    rs = view(OFF_RS + ki * 2048, 512, F32)
                    lo = max(tok_lo, c0)
                    hi = c0 + w
                    dcol = dst_lo + (lo - tok_lo)

                    def after(kst=kst, head=head, lo=lo, hi=hi, c0=c0, dcol=dcol, ki=ki):
                        if hi > lo:
                            pr.emit("sp", lambda e: e.dma_start(
                                out=KTd[head, :, dcol:dcol + (hi - lo)], in_=kst[:, lo - c0:hi - c0]),
                                reads=[kst_b[ki]], pwrites=[KTd_b[head]], dkey="kst%d" % ki)
                    tl_ = head_norm(pb, pB, w, 2, kst[:, 0:w], dict(writes=[kst_b[ki]]), (1.0 / DH, EPSC),
                                    rs[:, 0:w], rs_b[ki], after=after)
                    if pend_tail[0] is not None:
                        pend_tail[0]()
                    pend_tail[0] = tl_
        if pend_tail[0] is not None:
            pend_tail[0]()
            pend_tail[0] = None
        nch = (T - tok_lo) // 128
        for vp in range(8):
            i = load_wg_slot(wv3[:, :, vp * 256:(vp + 1) * 256], 256, (KC, 256))
            W3 = wgv(i).rearrange("p (a b) -> p a b", a=KC)
            for ch in range(nch):
                t0 = tok_lo + ch * 128
                tl = tiles_overlapping(t0, 128)
                k = st["gen"]
                st["gen"] += 1
                pb, pB = PS[4 + k % 4], ps_b[4 + k % 4]
                for kc in range(KC):
                    pr.emit("pe", lambda e, pb=pb, W3=W3, kc=kc, t0=t0: e.matmul(
                        pb[:, 0:256], Hap(kc, t0, 128), W3[:, kc, :], start=(kc == 0), stop=(kc == KC - 1)),
                        reads=[wg_b[i]] + [H[kc][x] for x in tl], writes=[pB])
                vi = cnt % 2
                cnt += 1
                vst = view(OFF_VST + vi * 512, 256, BF16)
                pr.emit("act", lambda e, vst=vst, pb=pb: e.activation(out=vst, in_=pb[:, 0:256], func=AF.Copy),
                        reads=[pB], writes=[vst_b[vi]])
                r0 = dst_lo + ch * 128
                pr.emit("sp", lambda e, vst=vst, r0=r0, vp=vp: e.dma_start(
                    out=Vd[r0:r0 + 128, vp * 256:(vp + 1) * 256], in_=vst),
                    reads=[vst_b[vi]], pwrites=[Vd_b], dkey="vst%d" % vi)

    def attn(lb):
        l = 2 + lb
        X, H, tiles, T = st["X"], st["H"], st["tiles"], st["T"]
        assert T == T_OWN
        rmsnorm(V_MIX + l)
        o = OFF_R
        OFF_QT = o; o += 32768
        OFF_KT = o; o += 2 * 3072
        OFF_VH = o; o += 2 * 3072
        OFF_BI = o; o += 2 * 2560
        OFF_TMP = o; o += 3 * 2560
        OFF_PT = o; o += 3 * 1280
        OFF_RD = o; o += 2 * 512
        OFF_RS = o; o += 2 * 2048
        assert o <= TOT
        qt_b = [[pr.sbuf(OFF_QT + (h * 1024 + c0) * 2, w * 2, "qt") for (c0, w) in tiles] for h in range(NH)]
        kt_b = [pr.sbuf(OFF_KT + i * 3072, 3072, "kt") for i in range(2)]
        vh_b = [pr.sbuf(OFF_VH + i * 3072, 3072, "vh") for i in range(2)]
        bi_b = [pr.sbuf(OFF_BI + i * 2560, 2560, "bi") for i in range(2)]
        tmp_b = [pr.sbuf(OFF_TMP + i * 2560, 2560, "tmp") for i in range(3)]
        pt_b = [pr.sbuf(OFF_PT + i * 1280, 1280, "pt") for i in range(3)]
        rd_b = [pr.sbuf(OFF_RD + i * 512, 512, "rd") for i in range(2)]
        rs_b = [pr.sbuf(OFF_RS + i * 2048, 2048, "rs") for i in range(2)]

        def QTap(h, c0, w):
            return view(OFF_QT + (h * 1024 + c0) * 2, w, BF16)

        wq3 = din["w_q"][lb].rearrange("(kc p) d -> p kc d", p=128)
        cnt = 0
        pend_tail = [None]
        for hp in range(8):
            i = load_wg_slot(wq3[:, :, hp * 256:(hp + 1) * 256], 256, (KC, 256))
            W3 = wgv(i).rearrange("p (a b) -> p a b", a=KC)
            for hh in range(2):
                head = hp * 2 + hh
                for tt, (c0, w) in enumerate(tiles):
                    k = st["gen"]
                    st["gen"] += 1
                    pb, pB = PS[4 + k % 4], ps_b[4 + k % 4]
                    for kc in range(KC):
                        pr.emit("pe", lambda e, pb=pb, W3=W3, kc=kc, hh=hh, c0=c0, w=w: e.matmul(
                            pb[:, 0:w], W3[:, kc, hh * 128:(hh + 1) * 128], Hap(kc, c0, w),
                            start=(kc == 0), stop=(kc == KC - 1)),
                            reads=[wg_b[i], H[kc][tt]], writes=[pB])
                    ri = cnt % 2
                    cnt += 1
                    rs = view(OFF_RS + ri * 2048, 512, F32)
                    tl_ = head_norm(pb, pB, w, lb, QTap(head, c0, w), dict(writes=[qt_b[head][tt]]),
                                    (1.0, EPSQ), rs[:, 0:w], rs_b[ri])
                    if pend_tail[0] is not None:
                        pend_tail[0]()
                    pend_tail[0] = tl_
        if pend_tail[0] is not None:
            pend_tail[0]()
            pend_tail[0] = None

        def load_head(h):
            i = h % 2
            ktv = view(OFF_KT + i * 3072, NKEY, BF16)
            vhv = view(OFF_VH + i * 3072, NKEY, BF16).rearrange("p (a b) -> p a b", a=12)
            biv = view(OFF_BI + i * 2560, 640, F32)
            pr.emit("sp", lambda e: e.dma_start(out=ktv, in_=KTd[h]), reads=[KTd_b[h]], writes=[kt_b[i]],
                    dkey="kt%d" % i)
            pr.emit("sp", lambda e: e.dma_start(
                out=vhv, in_=Vd[:, h * 128:(h + 1) * 128].rearrange("(a p) d -> p a d", p=128)),
                reads=[Vd_b], writes=[vh_b[i]], dkey="vh%d" % i)
            pr.emit("sp", lambda e: e.dma_start(out=biv, in_=din["biasG"][lb, h]), writes=[bi_b[i]],
                    dkey="bi%d" % i)
            pr.emit("pool", lambda e: e.tensor_tensor(out=biv, in0=biv, in1=MASKG, op=ALU.add),
                    reads=[bi_b[i], const_b], writes=[bi_b[i]])

        sc = dict(n=0)

        def scores(h, qt):
            i = h % 2
            n = sc["n"]
            sc["n"] += 1
            sa, sA = PS[(n % 3) * 2], ps_b[(n % 3) * 2]
            sb, sB = PS[(n % 3) * 2 + 1], ps_b[(n % 3) * 2 + 1]
            ktv = view(OFF_KT + i * 3072, NKEY, BF16)
            biv = view(OFF_BI + i * 2560, 640, F32)
            tmpv = view(OFF_TMP + (n % 3) * 2560, 640, F32)
            ptv = view(OFF_PT + (n % 3) * 1280, 640, BF16)
            q0 = qt * 128
            tq = q0 // 512
            for kb in range(5):
                dst = sa[:, kb * 128:(kb + 1) * 128] if kb < 4 else sb[:, 0:128]
                dB = sA if kb < 4 else sB
                k0 = (qt + kb) * 128
                pr.emit("pe", lambda e, dst=dst, k0=k0, q0=q0: e.matmul(
                    dst, ktv[:, k0:k0 + 128], QTap(h, q0, 128), start=True, stop=True),
                    reads=[kt_b[i], qt_b[h][tq]], pwrites=[dB])
            nh_ = max(0, 4 - qt)
            segs = []
            if nh_ > 0:
                segs.append((0, min(nh_, 4), HMASK))
            if nh_ < 4:
                segs.append((nh_, 4, ZCOL))
            first = True
            for (a, b, colap) in segs:
                kw = dict(writes=[tmp_b[n % 3]]) if first else dict(pwrites=[tmp_b[n % 3]])
                first = False
                pr.emit("dve", lambda e, a=a, b=b, colap=colap: e.scalar_tensor_tensor(
                    out=tmpv[:, a * 128:b * 128], in0=sa[:, a * 128:b * 128], scalar=colap[:, 0:1],
                    in1=biv[:, a * 128:b * 128], op0=ALU.add, op1=ALU.add),
                    reads=[sA, bi_b[i], const_b], **kw)
            pr.emit("dve", lambda e: e.tensor_tensor(
                out=tmpv[:, 512:640], in0=sb[:, 0:128], in1=biv[:, 512:640], op=ALU.add),
                reads=[sB, bi_b[i]], pwrites=[tmp_b[n % 3]])
            pr.emit("act", lambda e: e.activation(out=ptv, in_=tmpv, func=AF.Exp),
                    reads=[tmp_b[n % 3]], writes=[pt_b[n % 3]])
            return n

        def pv(h, qt, n):
            i = h % 2
            odb, oB = PS[6 + n % 2], ps_b[6 + n % 2]
            vhv = view(OFF_VH + i * 3072, NKEY, BF16).rearrange("p (a b) -> p a b", a=12)
            ptv = view(OFF_PT + (n % 3) * 1280, 640, BF16)
            rdv = view(OFF_RD + (n % 2) * 512, 128, F32)
            q0 = qt * 128
            tq = q0 // 512
            for kb in range(5):
                kw = dict(writes=[oB]) if kb == 0 else dict(pwrites=[oB])
                pr.emit("pe", lambda e, kb=kb: e.matmul(
                    odb[:, 0:128], vhv[:, qt + kb, :], ptv[:, kb * 128:(kb + 1) * 128],
                    start=(kb == 0), stop=(kb == 4)),
                    reads=[vh_b[i], pt_b[n % 3]], **kw)
            for kb in range(5):
                pr.emit("pe", lambda e, kb=kb: e.matmul(
                    odb[:, 128:256], ONES, ptv[:, kb * 128:(kb + 1) * 128],
                    start=(kb == 0), stop=(kb == 4)),
                    reads=[const_b, pt_b[n % 3]], pwrites=[oB])
            pr.emit("act", lambda e: e.activation(out=rdv, in_=odb[:, 128:256], func=AF.Ln),
                    reads=[oB], writes=[rd_b[n % 2]])
            pr.emit("act", lambda e: e.activation(out=rdv, in_=rdv, func=AF.Exp, scale=-1.0),
                    reads=[rd_b[n % 2]], writes=[rd_b[n % 2]])

            def tail():
                pr.emit("dve", lambda e: e.tensor_tensor(
                    out=Hap(h, q0, 128), in0=odb[:, 0:128], in1=rdv, op=ALU.mult),
                    reads=[oB, rd_b[n % 2]], pwrites=[H[h][tq]])
            return tail

        load_head(0)
        hist = []
        tails = []
        for h in range(NH):
            for qt in range(8):
                n = scores(h, qt)
                hist.append((h, qt, n))
                if tails:
                    tails.pop(0)()
                if len(hist) > 2:
                    tails.append(pv(*hist.pop(0)))
                if qt == 1 and h + 1 < NH:
                    load_head(h + 1)
        while hist or tails:
            if tails:
                tails.pop(0)()
            if hist:
                tails.append(pv(*hist.pop(0)))

        wo3 = din["w_o"][lb].rearrange("(kc p) d -> p kc d", p=128)
        for dp in range(8):
            i = load_wg_slot(wo3[:, :, dp * 256:(dp + 1) * 256], 256, (KC, 256))
            W3 = wgv(i).rearrange("p (a b) -> p a b", a=KC)
            for dd in range(2):
                dc = dp * 2 + dd
                for tt, (c0, w) in enumerate(tiles):
                    k = st["gen"]
                    st["gen"] += 1
                    pb, pB = PS[4 + k % 4], ps_b[4 + k % 4]
                    for kc in range(KC):
                        pr.emit("pe", lambda e, pb=pb, W3=W3, kc=kc, dd=dd, c0=c0, w=w: e.matmul(
                            pb[:, 0:w], W3[:, kc, dd * 128:(dd + 1) * 128], Hap(kc, c0, w),
                            start=(kc == 0), stop=(kc == KC - 1)),
                            reads=[wg_b[i], H[kc][tt]], writes=[pB])
                    pr.emit("dve", lambda e, pb=pb, dc=dc, c0=c0, w=w: e.tensor_tensor(
                        out=Xap(dc, c0, w), in0=pb[:, 0:w], in1=Xap(dc, c0, w), op=ALU.add),
                        reads=[pB, X[dc][tt]], writes=[X[dc][tt]])

    setup()
    own_tiles = [(0, 512), (512, 512)]
    halo_tiles = [(0, 272), (272, 272)]
    for pi, stp in enumerate(plan):
        kind = stp[0]
        if kind == "load_own":
            load_x(din["xT"], T_OWN, own_tiles)
        elif kind == "load_halo":
            load_x(din["xhT"], T_HALO, halo_tiles)
        elif kind == "ffn":
            ffn(stp[1], stp[2])
        elif kind == "pool":
            pool(stp[1], stp[2])
        elif kind == "kv_halo":
            nxt = plan[pi + 1][0] if pi + 1 < len(plan) else None
            if nxt == "load_own":
                kv(T_HALO - HALO_KV, 0,
                   after_norm=lambda: st.__setitem__("xpre", prefetch_x(din["xT"], T_OWN, own_tiles)))
            else:
                kv(T_HALO - HALO_KV, 0)
        elif kind == "kv_own":
            kv(0, HALO_KV)
        elif kind == "attn":
            attn(stp[1])
        elif kind == "store":
            store_x()
        else:
            raise ValueError(kind)
    pr.lower(nc)
    nc._in_names = list(din.keys())
    nc._n_ops = {e: len(pr.ops[e]) for e in ENGS}
    return nc


FULL_PLAN = (
    [("load_halo",)]
    + [s for l in (0, 1) for s in (("ffn", l, 1), ("pool", l, True), ("ffn", l, 2))]
    + [("kv_halo",), ("load_own",)]
    + [s for l in (0, 1) for s in (("ffn", l, 1), ("pool", l, False), ("ffn", l, 2))]
    + [("kv_own",)]
    + [s for lb in (0, 1) for s in (("ffn", 2 + lb, 1), ("attn", lb), ("ffn", 2 + lb, 2))]
    + [("store",)]
)


def _vec_layout(v):
    return np.ascontiguousarray(np.asarray(v, np.float32).reshape(KC, 128).T)


def host_prep(inputs):
    x = np.asarray(inputs["x"], np.float32)[0]
    vecs = np.zeros((128, NVEC * KC), np.float32)

    def put(idx, v):
        vecs[:, idx * KC:(idx + 1) * KC] = _vec_layout(v)
    for l in range(4):
        put(V_FFN1 + l, inputs["ffn1_norm"][l])
        put(V_MIX + l, inputs["mix_norm"][l])
        put(V_FFN2 + l, inputs["ffn2_norm"][l])
    put(V_KV, inputs["kv_norm"])
    for l in range(2):
        put(V_PSC + l, inputs["pool_scale"][l])
    hg = np.zeros((128, 4), np.float32)
    hg[:, 0] = np.asarray(inputs["q_gain"], np.float32)[0]
    hg[:, 1] = np.asarray(inputs["q_gain"], np.float32)[1]
    hg[:, 2] = np.asarray(inputs["k_gain"], np.float32)
    m = np.arange(128)[:, None, None]
    kb = np.arange(5)[None, :, None]
    r = np.arange(128)[None, None, :]
    kk = kb * 128 + m
    rel = (512 + r) - kk
    idx = np.clip(rel, -63, 128) + 63
    qc = 8 + r // 64
    kc_ = kk // 64
    valid = (qc - kc_ >= 0) & (qc - kc_ <= 8)
    maskG = np.where(valid, 0.0, NEG).astype(np.float32).reshape(128, 640)
    rb = np.asarray(inputs["rel_bias"], np.float32)
    biasG = np.ascontiguousarray(rb[:, :, idx].reshape(2, NH, 128, 640))
    xT = np.ascontiguousarray(x.T)
    shared = {"vecs": vecs, "hg": hg, "maskG": maskG, "biasG": biasG}
    for nm in ("ffn1_w_gate", "ffn1_w_up", "ffn1_w_down", "ffn2_w_gate", "ffn2_w_up", "ffn2_w_down",
               "pool_w", "w_k", "w_v", "w_q", "w_o"):
        shared[nm] = np.ascontiguousarray(np.asarray(inputs[nm], np.float32))
    maps = []
    for c in range(NCORES):
        t0 = c * T_OWN
        d = dict(shared)
        d["xT"] = np.ascontiguousarray(xT[:, t0:t0 + T_OWN])
        xh = np.zeros((D, T_HALO), np.float32)
        lo = t0 - T_HALO
        if lo >= 0:
            xh[:, :] = xT[:, lo:t0]
        elif t0 > 0:
            xh[:, -t0:] = xT[:, 0:t0]
        d["xhT"] = xh
        d["hmask"] = np.full((128, 1), 0.0 if c > 0 else NEG, np.float32)
        invc = np.zeros((128, 64), np.float32)
        for g, wdw in enumerate(POOL_W):
            for t in range(16):
                cntv = min(t0 + t + 1, wdw)
                invc[:, g * 16 + t] = 1.0 / cntv
        d["invc"] = invc
        maps.append(d)
    return maps


_NC_CACHE = {}


def kernel(**inputs):
    maps = host_prep(inputs)
    if "full" not in _NC_CACHE:
        _NC_CACHE["full"] = build(FULL_PLAN)
    nc = _NC_CACHE["full"]
    maps = [{k: m[k] for k in nc._in_names} for m in maps]
    res = run_bass_kernel_spmd(nc, maps, core_ids=list(range(NCORES)))
    outs = [np.asarray(r["outT"], np.float32) for r in res.results]
    full = np.concatenate(outs, axis=1)
    return np.ascontiguousarray(full.T)[None].astype(np.float32)
```

```python
import numpy as np
import concourse.bass as bass
import concourse.mybir as mybir
from concourse.bass_utils import run_bass_kernel_spmd

F32 = mybir.dt.float32
BF16 = mybir.dt.bfloat16
AF = mybir.ActivationFunctionType
ALU = mybir.AluOpType

NCORES = 8
D = 2048
KC = 16
DFF = 5504
NFC = 43
NH = 16
DH = 128
SEQ = 8192
T_OWN = 1024
HALO_KV = 512
T_HALO = 544
NKEY = T_OWN + HALO_KV
EPS = 1e-6
NEG = -30000.0
POOL_W = (2, 4, 8, 16)
NS = 3
FG = 2

V_FFN1 = 0
V_MIX = 4
V_FFN2 = 8
V_KV = 12
V_PSC = 13
NVEC = 15

ENGS = ("pe", "act", "dve", "pool", "sp")


class Op:
    __slots__ = ("eng", "fn", "deps", "idx", "dkey", "val", "inc")


class Buf:
    __slots__ = ("w", "r", "off", "size", "name")

    def __init__(self, name=""):
        self.w = {}
        self.r = {}
        self.off = None
        self.size = 0
        self.name = name


class Prog:
    def __init__(self):
        self.ops = {e: [] for e in ENGS}
        self.live = []
        self.final = []

    def sbuf(self, off, size, name=""):
        b = Buf(name)
        b.off, b.size = off, size
        keep = []
        for o in self.live:
            if o.off < off + size and off < o.off + o.size:
                for k, op in o.w.items():
                    b.r[("a", id(op))] = op
                for k, op in o.r.items():
                    b.r[("a", id(op))] = op
            else:
                keep.append(o)
        keep.append(b)
        self.live = keep
        return b

    def emit(self, eng, fn, reads=(), writes=(), pwrites=(), dkey=None):
        op = Op()
        op.eng, op.fn, op.dkey = eng, fn, dkey
        op.inc = dkey is not None
        op.val = 0
        op.idx = len(self.ops[eng])
        deps = {}
        for b in reads:
            for o in b.w.values():
                deps[id(o)] = o
        for b in writes:
            for o in b.w.values():
                deps[id(o)] = o
            for o in b.r.values():
                deps[id(o)] = o
        for b in pwrites:
            for o in b.r.values():
                deps[id(o)] = o
        op.deps = list(deps.values())
        key = dkey if dkey else eng
        for b in reads:
            if b not in writes:
                b.r[key] = op
        for b in writes:
            b.w = {key: op}
            b.r = {}
        for b in pwrites:
            b.w[key] = op
        self.ops[eng].append(op)
        return op

    @staticmethod
    def _needs_wait(op, d):
        if d.dkey is None and op.dkey is None and d.eng == op.eng:
            if op.eng == "pe":
                return False
            return (op.idx - d.idx) <= 2
        return True

    def lower(self, nc):
        for e in ENGS:
            for op in self.ops[e]:
                for d in op.deps:
                    if self._needs_wait(op, d):
                        d.inc = True
        dcount = {}
        for e in ENGS:
            c = 0
            for op in self.ops[e]:
                if op.dkey is None:
                    if op.inc:
                        c += 1
                    op.val = c
                else:
                    dcount[op.dkey] = dcount.get(op.dkey, 0) + 16
                    op.val = dcount[op.dkey]
        esem = {e: nc.alloc_semaphore("s_" + e) for e in ENGS}
        dsem = {k: nc.alloc_semaphore("d_" + k) for k in dcount}
        finals = self.final

        def run(e, eng):
            waited = {}
            for op in self.ops[e]:
                for d in op.deps:
                    if not self._needs_wait(op, d):
                        continue
                    key = ("d", d.dkey) if d.dkey else ("e", d.eng)
                    if waited.get(key, 0) >= d.val:
                        continue
                    sem = dsem[d.dkey] if d.dkey else esem[d.eng]
                    eng.wait_ge(sem, d.val)
                    waited[key] = d.val
                ins = op.fn(eng)
                if op.dkey:
                    ins.then_inc(dsem[op.dkey], 16)
                elif op.inc:
                    ins.then_inc(esem[e], 1)
            if e == "sp":
                for op in finals:
                    key = ("d", op.dkey)
                    if waited.get(key, 0) < op.val:
                        eng.wait_ge(dsem[op.dkey], op.val)
                        waited[key] = op.val

        with nc.Block() as block:
            @block.tensor
            def _(eng):
                run("pe", eng)

            @block.scalar
            def _(eng):
                run("act", eng)

            @block.vector
            def _(eng):
                run("dve", eng)

            @block.gpsimd
            def _(eng):
                run("pool", eng)

            @block.sync
            def _(eng):
                run("sp", eng)


def build(plan, kv_kind="Internal"):
    nc = bass.Bass("TRN2", target_bir_lowering=False)
    pr = Prog()
    SHAPES = {"xT": [D, T_OWN], "xhT": [D, T_HALO], "vecs": [128, NVEC * KC], "hg": [128, 4],
              "hmask": [128, 1], "invc": [128, 64], "maskG": [128, 640], "biasG": [2, NH, 128, 640],
              "ffn1_w_gate": [4, D, DFF], "ffn1_w_up": [4, D, DFF], "ffn2_w_gate": [4, D, DFF],
              "ffn2_w_up": [4, D, DFF], "ffn1_w_down": [4, DFF, D], "ffn2_w_down": [4, DFF, D],
              "pool_w": [2, 4, 512, 512], "w_k": [D, D], "w_v": [D, D], "w_q": [2, D, D], "w_o": [2, D, D]}

    class _Lazy(dict):
        def __missing__(self, name):
            self[name] = nc.dram_tensor(name, list(SHAPES[name]), F32, kind="ExternalInput").ap()
            return self[name]
    din = _Lazy()
    outT = nc.dram_tensor("outT", [D, T_OWN], F32, kind="ExternalOutput").ap()
    kvk = kv_kind
    KTd = nc.dram_tensor("KTd", [NH, 128, NKEY], BF16, kind=kvk).ap()
    Vd = nc.dram_tensor("Vd", [NKEY, D], BF16, kind=kvk).ap()
    KTd_b = [Buf("KTd%d" % h) for h in range(NH)]
    Vd_b = Buf("Vd")

    TOT = 212000
    big = nc.alloc_sbuf_tensor("big", [128, TOT // 4], F32)

    def view(off, n, dt):
        assert off % 4 == 0
        if dt == F32:
            return big[:, off // 4: off // 4 + n]
        nb = n * 2
        assert nb % 4 == 0
        return big[:, off // 4: off // 4 + nb // 4].bitcast(BF16)

    OFF_X = 0
    OFF_H = 65536
    OFF_C = 98304
    o = OFF_C
    OFF_VEC = o; o += NVEC * KC * 4
    OFF_HG = o; o += 16
    OFF_HMASK = o; o += 4
    OFF_ZCOL = o; o += 4
    OFF_EPSC = o; o += 4
    OFF_EPSQ = o; o += 4
    OFF_INVC = o; o += 256
    OFF_MASKG = o; o += 2560
    OFF_ONES = o; o += 256
    OFF_PHALO = o; o += 2 * KC * 16 * 4
    OFF_ZERO16 = o; o += 64
    OFF_RSTD = o; o += 4096
    OFF_SQ = o; o += 3 * 1024
    OFF_WG = o; o += NS * 8192
    OFF_R = o
    assert OFF_R + 61440 <= TOT, OFF_R

    VEC = view(OFF_VEC, NVEC * KC, F32)
    HG = view(OFF_HG, 4, F32)
    HMASK = view(OFF_HMASK, 1, F32)
    ZCOL = view(OFF_ZCOL, 1, F32)
    EPSC = view(OFF_EPSC, 1, F32)
    EPSQ = view(OFF_EPSQ, 1, F32)
    INVC = view(OFF_INVC, 64, F32)
    MASKG = view(OFF_MASKG, 640, F32)
    ONES = view(OFF_ONES, 128, BF16)
    PHALO = view(OFF_PHALO, 2 * KC * 16, F32)
    ZERO16 = view(OFF_ZERO16, 16, F32)
    RSTD = view(OFF_RSTD, 1024, F32)
    const_b = pr.sbuf(OFF_VEC, OFF_PHALO - OFF_VEC, "const")
    phalo_b = [[pr.sbuf(OFF_PHALO + (l * KC + kc) * 64, 64, "ph") for kc in range(KC)] for l in range(2)]
    zero16_b = pr.sbuf(OFF_ZERO16, 64, "z16")
    SQ = [view(OFF_SQ + i * 1024, 512, BF16) for i in range(3)]
    sq_b = [pr.sbuf(OFF_SQ + i * 1024, 1024, "sq") for i in range(3)]
    wg_b = [pr.sbuf(OFF_WG + i * 8192, 8192, "wg%d" % i) for i in range(NS)]

    def wgv(i):
        return view(OFF_WG + i * 8192, 4096, BF16)

    PS = [nc.alloc_psum_tensor("ps%d" % i, [128, 512], F32) for i in range(8)]
    ps_b = [Buf("ps%d" % i) for i in range(8)]

    st = dict(sq=0, wg=0, gen=0, ss=0, T=None, tiles=None, X=None, H=None, rstd=None)

    def Xap(kc, c0, w):
        return view(OFF_X + (kc * 1024 + c0) * 4, w, F32)

    def Hap(kc, c0, w):
        return view(OFF_H + (kc * 1024 + c0) * 2, w, BF16)

    def mk_xbufs(tiles):
        return [[pr.sbuf(OFF_X + (kc * 1024 + c0) * 4, w * 4, "x") for (c0, w) in tiles] for kc in range(KC)]

    def set_pass(T, tiles, xbufs=None):
        st["T"], st["tiles"] = T, tiles
        st["X"] = xbufs if xbufs is not None else mk_xbufs(tiles)
        st["H"] = [[pr.sbuf(OFF_H + (kc * 1024 + c0) * 2, w * 2, "h") for (c0, w) in tiles] for kc in range(KC)]
        st["rstd"] = [pr.sbuf(OFF_RSTD + c0 * 4, w * 4, "rstd") for (c0, w) in tiles]

    def tiles_overlapping(c0, w):
        return [i for i, (a, b) in enumerate(st["tiles"]) if a < c0 + w and c0 < a + b]

    def setup():
        pr.emit("sp", lambda e: e.dma_start(out=VEC, in_=din["vecs"]), writes=[const_b], dkey="c0")
        pr.emit("sp", lambda e: e.dma_start(out=HG, in_=din["hg"]), pwrites=[const_b], dkey="c1")
        pr.emit("sp", lambda e: e.dma_start(out=HMASK, in_=din["hmask"]), pwrites=[const_b], dkey="c2")
        pr.emit("sp", lambda e: e.dma_start(out=INVC, in_=din["invc"]), pwrites=[const_b], dkey="c3")
        pr.emit("sp", lambda e: e.dma_start(out=MASKG, in_=din["maskG"]), pwrites=[const_b], dkey="c4")
        pr.emit("dve", lambda e: e.memset(ONES, 1.0), pwrites=[const_b])
        pr.emit("dve", lambda e: e.memset(ZCOL, 0.0), pwrites=[const_b])
        pr.emit("dve", lambda e: e.memset(EPSC, EPS), pwrites=[const_b])
        pr.emit("dve", lambda e: e.memset(EPSQ, EPS * DH), pwrites=[const_b])
        pr.emit("dve", lambda e: e.memset(ZERO16, 0.0), writes=[zero16_b])

    def prefetch_x(src, T, tiles):
        xb = mk_xbufs(tiles)
        s3 = src.rearrange("(kc p) t -> p kc t", p=128)
        for tt, (c0, w) in enumerate(tiles):
            for q in range(4):
                dst = view(OFF_X + q * 4 * 4096, 4 * 1024, F32).rearrange("p (k t) -> p k t", k=4)[:, :, c0:c0 + w]
                bl = [xb[kc][tt] for kc in range(q * 4, q * 4 + 4)]
                pr.emit("sp", lambda e, dst=dst, q=q, c0=c0, w=w: e.dma_start(
                    out=dst, in_=s3[:, q * 4:(q + 1) * 4, c0:c0 + w]),
                    writes=bl, dkey="xin%d" % (tt * 4 + q))
        return xb

    def load_x(src, T, tiles):
        xb = st.pop("xpre", None)
        if xb is None:
            xb = prefetch_x(src, T, tiles)
        set_pass(T, tiles, xb)

    def store_x():
        T, tiles = st["T"], st["tiles"]
        d3 = outT.rearrange("(kc p) t -> p kc t", p=128)
        for tt, (c0, w) in enumerate(tiles):
            for q in range(4):
                srcv = view(OFF_X + q * 4 * 4096, 4 * 1024, F32).rearrange("p (k t) -> p k t", k=4)[:, :, c0:c0 + w]
                bl = [st["X"][kc][tt] for kc in range(q * 4, q * 4 + 4)]
                op = pr.emit("sp", lambda e, srcv=srcv, q=q, c0=c0, w=w: e.dma_start(
                    out=d3[:, q * 4:(q + 1) * 4, c0:c0 + w], in_=srcv),
                    reads=bl, dkey="xout%d" % (tt * 4 + q))
                pr.final.append(op)

    def rmsnorm(vidx, write_h=True):
        X, H = st["X"], st["H"]
        for tt, (c0, w) in enumerate(st["tiles"]):
            ssi = st["ss"] % 2
            st["ss"] += 1
            ssb, ssB = PS[ssi], ps_b[ssi]
            for kc in range(KC):
                si = st["sq"] % 3
                st["sq"] += 1
                pr.emit("act", lambda e, si=si, kc=kc, c0=c0, w=w: e.activation(
                    out=SQ[si][:, 0:w], in_=Xap(kc, c0, w), func=AF.Square),
                    reads=[X[kc][tt]], writes=[sq_b[si]])
                pr.emit("pe", lambda e, si=si, kc=kc, w=w, ssb=ssb: e.matmul(
                    ssb[:, 0:w], ONES, SQ[si][:, 0:w], start=(kc == 0), stop=(kc == KC - 1)),
                    reads=[sq_b[si], const_b], writes=[ssB])
            rs = RSTD[:, c0:c0 + w]
            pr.emit("act", lambda e, rs=rs, ssb=ssb, w=w: e.activation(
                out=rs, in_=ssb[:, 0:w], func=AF.Ln, bias=EPSC[:, 0:1], scale=1.0 / D),
                reads=[ssB, const_b], writes=[st["rstd"][tt]])
            pr.emit("act", lambda e, rs=rs: e.activation(out=rs, in_=rs, func=AF.Exp, scale=-0.5),
                    reads=[st["rstd"][tt]], writes=[st["rstd"][tt]])
            for kc in range(KC if write_h else 0):
                col = vidx * KC + kc
                pr.emit("dve", lambda e, kc=kc, c0=c0, w=w, col=col, rs=rs: e.scalar_tensor_tensor(
                    out=Hap(kc, c0, w), in0=Xap(kc, c0, w), scalar=VEC[:, col:col + 1], in1=rs,
                    op0=ALU.mult, op1=ALU.mult),
                    reads=[X[kc][tt], st["rstd"][tt], const_b], writes=[H[kc][tt]])

    def load_wg_slot(src3, ncols_total, shape3):
        i = st["wg"] % NS
        st["wg"] += 1
        a, b = shape3
        dst = wgv(i)[:, 0:a * b].rearrange("p (a b) -> p a b", a=a)
        pr.emit("pool", lambda e, dst=dst, src3=src3: e.dma_start(out=dst, in_=src3),
                writes=[wg_b[i]], dkey="wg%d" % i)
        return i

    def ffn(l, which):
        X, H, tiles = st["X"], st["H"], st["tiles"]
        nt = len(tiles)
        pre = "ffn%d_" % which
        wg3 = din[pre + "w_gate"][l].rearrange("(kc p) f -> p kc f", p=128)
        wu3 = din[pre + "w_up"][l].rearrange("(kc p) f -> p kc f", p=128)
        wd3 = din[pre + "w_down"][l].rearrange("(fc p) d -> p fc d", p=128)
        rmsnorm((V_FFN1 if which == 1 else V_FFN2) + l)
        OFF_WU = OFF_R
        OFF_WD = OFF_WU + NS * 8192
        OFF_ACT = OFF_WD + NS * 8192
        OFF_SG = OFF_ACT + 2 * 4096
        wu_b = [pr.sbuf(OFF_WU + i * 8192, 8192, "wu") for i in range(NS)]
        wd_b = [pr.sbuf(OFF_WD + i * 8192, 8192, "wd") for i in range(NS)]
        act_b = [[[pr.sbuf(OFF_ACT + s * 4096 + (j * 1024 + c0) * 2, w * 2, "act") for (c0, w) in tiles]
                  for j in range(FG)] for s in range(2)]
        sg_b = [pr.sbuf(OFF_SG + i * 2048, 2048, "sg") for i in range(2)]

        def WU3(i):
            return view(OFF_WU + i * 8192, 4096, BF16).rearrange("p (a b) -> p a b", a=KC)

        def WG3(i):
            return wgv(i).rearrange("p (a b) -> p a b", a=KC)

        def WD3(i):
            return view(OFF_WD + i * 8192, 4096, BF16).rearrange("p (a b) -> p a b", a=FG)

        def ACTap(s, j, c0, w):
            return view(OFF_ACT + s * 4096 + (j * 1024 + c0) * 2, w, BF16)

        def SGap(i, w):
            return view(OFF_SG + i * 2048, w, F32)

        groups = []
        f = 0
        while f < NFC:
            n = min(FG, NFC - f)
            groups.append((f, n))
            f += n
        slots = {}
        cnt = dict(gu=0, d=0, ring=0)

        def load_group(g):
            f0, n = groups[g]
            gi = st["wg"] % NS
            st["wg"] += 1
            ri = cnt["ring"] % NS
            cnt["ring"] += 1
            fw = n * 128
            pr.emit("pool", lambda e, gi=gi, f0=f0, fw=fw: e.dma_start(
                out=WG3(gi)[:, :, 0:fw], in_=wg3[:, :, f0 * 128:f0 * 128 + fw]),
                writes=[wg_b[gi]], dkey="wg%d" % gi)
            pr.emit("pool", lambda e, ri=ri, f0=f0, fw=fw: e.dma_start(
                out=WU3(ri)[:, :, 0:fw], in_=wu3[:, :, f0 * 128:f0 * 128 + fw]),
                writes=[wu_b[ri]], dkey="wu%d" % ri)
            pr.emit("pool", lambda e, ri=ri, f0=f0, n=n: e.dma_start(
                out=WD3(ri)[:, 0:n, :], in_=wd3[:, f0:f0 + n, :]),
                writes=[wd_b[ri]], dkey="wd%d" % ri)
            slots[g] = (gi, ri)

        def gu_tile(g, j, tt, dl=()):
            gi, ri = slots[g]
            c0, w = tiles[tt]
            k = cnt["gu"]
            cnt["gu"] += 1
            gb, gB = PS[k % 2], ps_b[k % 2]
            ub, uB = PS[2 + k % 2], ps_b[2 + k % 2]
            dl = list(dl)
            nseg = 4
            per = (len(dl) + nseg - 1) // nseg
            segs = [dl[i * per:(i + 1) * per] for i in range(nseg)]

            def gate(kc):
                pr.emit("pe", lambda e, kc=kc: e.matmul(
                    gb[:, 0:w], WG3(gi)[:, kc, j * 128:(j + 1) * 128], Hap(kc, c0, w),
                    start=(kc == 0), stop=(kc == KC - 1)),
                    reads=[wg_b[gi], H[kc][tt]], writes=[gB])

            def up(kc):
                pr.emit("pe", lambda e, kc=kc: e.matmul(
                    ub[:, 0:w], WU3(ri)[:, kc, j * 128:(j + 1) * 128], Hap(kc, c0, w),
                    start=(kc == 0), stop=(kc == KC - 1)),
                    reads=[wu_b[ri], H[kc][tt]], writes=[uB])
            for kc in range(0, 8):
                gate(kc)
            for a in segs[0]:
                d_tile(*a)
            for kc in range(8, KC):
                gate(kc)
            pr.emit("act", lambda e: e.activation(
                out=SGap(k % 2, w), in_=gb[:, 0:w], func=AF.Silu),
                reads=[gB], writes=[sg_b[k % 2]])
            for a in segs[1]:
                d_tile(*a)
            for kc in range(0, 8):
                up(kc)
            for a in segs[2]:
                d_tile(*a)
            for kc in range(8, KC):
                up(kc)
            pr.emit("dve", lambda e: e.tensor_tensor(
                out=ACTap(g % 2, j, c0, w), in0=SGap(k % 2, w), in1=ub[:, 0:w], op=ALU.mult),
                reads=[sg_b[k % 2], uB], writes=[act_b[g % 2][j][tt]])
            for a in segs[3]:
                d_tile(*a)

        def d_tile(g, dc, tt):
            gi, ri = slots[g]
            f0, n = groups[g]
            c0, w = tiles[tt]
            k = cnt["d"]
            cnt["d"] += 1
            db, dB = PS[4 + k % 4], ps_b[4 + k % 4]
            for j in range(n):
                pr.emit("pe", lambda e, j=j, db=db, ri=ri, dc=dc, g=g, c0=c0, w=w, n=n: e.matmul(
                    db[:, 0:w], WD3(ri)[:, j, dc * 128:(dc + 1) * 128], ACTap(g % 2, j, c0, w),
                    start=(j == 0), stop=(j == n - 1)),
                    reads=[wd_b[ri], act_b[g % 2][j][tt]], writes=[dB])
            pr.emit("dve", lambda e, db=db, dc=dc, c0=c0, w=w: e.scalar_tensor_tensor(
                out=Xap(dc, c0, w), in0=db[:, 0:w], scalar=0.5, in1=Xap(dc, c0, w),
                op0=ALU.mult, op1=ALU.add),
                reads=[dB, X[dc][tt]], writes=[X[dc][tt]])

        ng = len(groups)
        for g in range(ng + 1):
            gu_list = []
            d_list = []
            if g < ng:
                load_group(g)
                gu_list = [(g, j, tt) for j in range(groups[g][1]) for tt in range(nt)]
            if g >= 1:
                d_list = [(g - 1, dc, tt) for tt in range(nt) for dc in range(KC)]
            if not gu_list:
                for a in d_list:
                    d_tile(*a)
                continue
            per = (len(d_list) + len(gu_list) - 1) // len(gu_list) if d_list else 0
            for ii, a in enumerate(gu_list):
                gu_tile(*a, dl=d_list[ii * per:(ii + 1) * per])

    def pool(l, halo_mode):
        X, H, tiles, T = st["X"], st["H"], st["tiles"], st["T"]
        nt = len(tiles)
        rmsnorm(V_MIX + l, write_h=False)
        LW = (16 + 1024) * 4
        OFF_HC = OFF_R
        OFF_SA = OFF_HC + 2 * LW
        OFF_T16 = OFF_SA + 2 * LW
        hc_b = [pr.sbuf(OFF_HC + i * LW, LW, "hc") for i in range(2)]
        sa_b = [pr.sbuf(OFF_SA + i * LW, LW, "sa") for i in range(2)]
        t16_b = pr.sbuf(OFF_T16, 64, "t16")
        T16 = view(OFF_T16, 16, F32)

        def HC(i):
            return view(OFF_HC + i * LW, 16 + 1024, F32)

        def SA(i):
            return view(OFF_SA + i * LW, 16 + 1024, F32)

        W = 16 + T
        for dc in range(KC):
            g = dc // 4
            win = POOL_W[g]
            hi = dc % 2
            hc, hcB = HC(hi), hc_b[hi]
            col = (V_MIX + l) * KC + dc
            xall = view(OFF_X + dc * 4096, T, F32)
            pr.emit("dve", lambda e, hc=hc, xall=xall, col=col, T=T: e.scalar_tensor_tensor(
                out=hc[:, 16:16 + T], in0=xall, scalar=VEC[:, col:col + 1], in1=RSTD[:, 0:T],
                op0=ALU.mult, op1=ALU.mult),
                reads=[X[dc][i] for i in range(nt)] + st["rstd"] + [const_b], pwrites=[hcB])
            if halo_mode:
                pr.emit("act", lambda e, hc=hc: e.activation(out=hc[:, 0:16], in_=ZERO16, func=AF.Copy),
                        reads=[zero16_b], pwrites=[hcB])
            else:
                ph = PHALO[:, (l * KC + dc) * 16:(l * KC + dc + 1) * 16]
                pr.emit("act", lambda e, hc=hc, ph=ph: e.activation(out=hc[:, 0:16], in_=ph, func=AF.Copy),
                        reads=[phalo_b[l][dc]], pwrites=[hcB])
            src, srcB = hc, hcB
            k = 1
            si = 0
            v0 = 0
            while k < win:
                dst, dstB = SA(si), sa_b[si]
                v1 = v0 + k
                pr.emit("dve", lambda e, dst=dst, src=src, k=k, W=W, v1=v1: e.tensor_tensor(
                    out=dst[:, v1:W], in0=src[:, v1:W], in1=src[:, v1 - k:W - k], op=ALU.add),
                    reads=[srcB], writes=[dstB])
                src, srcB = dst, dstB
                si ^= 1
                v0 = v1
                k *= 2
            pr.emit("dve", lambda e, src=src, hc=hc, dc=dc, T=T, win=win: e.scalar_tensor_tensor(
                out=Hap(dc, 0, T), in0=src[:, 16:16 + T], scalar=1.0 / win, in1=hc[:, 16:16 + T],
                op0=ALU.mult, op1=ALU.subtract),
                reads=[srcB, hcB], writes=[H[dc][i] for i in range(nt)])
            pr.emit("dve", lambda e, src=src, g=g: e.tensor_tensor(
                out=T16, in0=src[:, 16:32], in1=INVC[:, g * 16:(g + 1) * 16], op=ALU.mult),
                reads=[srcB, const_b], writes=[t16_b])
            pr.emit("dve", lambda e, hc=hc, dc=dc: e.tensor_tensor(
                out=Hap(dc, 0, 16), in0=T16, in1=hc[:, 16:32], op=ALU.subtract),
                reads=[t16_b, hcB], pwrites=[H[dc][0]])
            if halo_mode:
                ph = PHALO[:, (l * KC + dc) * 16:(l * KC + dc + 1) * 16]
                pr.emit("act", lambda e, hc=hc, ph=ph, T=T: e.activation(out=ph, in_=hc[:, T:T + 16], func=AF.Copy),
                        reads=[hcB], writes=[phalo_b[l][dc]])
        for g in range(4):
            src3 = din["pool_w"][l, g].rearrange("(kc p) d -> p kc d", p=128)
            i = load_wg_slot(src3, 512, (4, 512))
            W3 = wgv(i)[:, 0:2048].rearrange("p (a b) -> p a b", a=4)
            for dco in range(4):
                dc = g * 4 + dco
                col = (V_PSC + l) * KC + dc
                for tt, (c0, w) in enumerate(tiles):
                    k = st["gen"]
                    st["gen"] += 1
                    pb, pB = PS[4 + k % 4], ps_b[4 + k % 4]
                    for kc in range(4):
                        pr.emit("pe", lambda e, pb=pb, W3=W3, kc=kc, dco=dco, g=g, c0=c0, w=w: e.matmul(
                            pb[:, 0:w], W3[:, kc, dco * 128:(dco + 1) * 128], Hap(g * 4 + kc, c0, w),
                            start=(kc == 0), stop=(kc == 3)),
                            reads=[wg_b[i], H[g * 4 + kc][tt]], writes=[pB])
                    pr.emit("dve", lambda e, pb=pb, dc=dc, c0=c0, w=w, col=col: e.scalar_tensor_tensor(
                        out=Xap(dc, c0, w), in0=pb[:, 0:w], scalar=VEC[:, col:col + 1], in1=Xap(dc, c0, w),
                        op0=ALU.mult, op1=ALU.add),
                        reads=[pB, X[dc][tt], const_b], writes=[X[dc][tt]])

    def head_norm(pb, pB, w, gcol, out_ap, out_bufs, post_scale, rs_ap, rs_b, after=None):
        si = st["sq"] % 3
        st["sq"] += 1
        pr.emit("act", lambda e: e.activation(out=SQ[si][:, 0:w], in_=pb[:, 0:w], func=AF.Square),
                reads=[pB], writes=[sq_b[si]])

        def tail():
            ssi = st["ss"] % 2
            st["ss"] += 1
            ssb, ssB = PS[ssi], ps_b[ssi]
            pr.emit("pe", lambda e: e.matmul(ssb[:, 0:w], ONES, SQ[si][:, 0:w], start=True, stop=True),
                    reads=[sq_b[si], const_b], writes=[ssB])
            pr.emit("act", lambda e: e.activation(
                out=rs_ap, in_=ssb[:, 0:w], func=AF.Ln, bias=post_scale[1][:, 0:1], scale=post_scale[0]),
                reads=[ssB, const_b], writes=[rs_b])
            pr.emit("act", lambda e: e.activation(out=rs_ap, in_=rs_ap, func=AF.Exp, scale=-0.5),
                    reads=[rs_b], writes=[rs_b])
            pr.emit("dve", lambda e: e.scalar_tensor_tensor(
                out=out_ap, in0=pb[:, 0:w], scalar=HG[:, gcol:gcol + 1], in1=rs_ap, op0=ALU.mult, op1=ALU.mult),
                reads=[pB, rs_b, const_b], **out_bufs)
            if after is not None:
                after()
        return tail

    def kv(tok_lo, dst_lo, after_norm=None):
        X, H, tiles, T = st["X"], st["H"], st["tiles"], st["T"]
        rmsnorm(V_KV)
        if after_norm is not None:
            after_norm()
        OFF_KST = OFF_R
        OFF_VST = OFF_KST + 2 * 1024
        OFF_RS = OFF_VST + 2 * 512
        kst_b = [pr.sbuf(OFF_KST + i * 1024, 1024, "kst") for i in range(2)]
        vst_b = [pr.sbuf(OFF_VST + i * 512, 512, "vst") for i in range(2)]
        rs_b = [pr.sbuf(OFF_RS + i * 2048, 2048, "rs") for i in range(2)]
        cnt = 0
        pend_tail = [None]
        wk3 = din["w_k"].rearrange("(kc p) d -> p kc d", p=128)
        wv3 = din["w_v"].rearrange("(kc p) d -> p kc d", p=128)
        for hp in range(8):
            i = load_wg_slot(wk3[:, :, hp * 256:(hp + 1) * 256], 256, (KC, 256))
            W3 = wgv(i).rearrange("p (a b) -> p a b", a=KC)
            for hh in range(2):
                head = hp * 2 + hh
                for tt, (c0, w) in enumerate(tiles):
                    k = st["gen"]
                    st["gen"] += 1
                    pb, pB = PS[4 + k % 4], ps_b[4 + k % 4]
                    for kc in range(KC):
                        pr.emit("pe", lambda e, pb=pb, W3=W3, kc=kc, hh=hh, c0=c0, w=w: e.matmul(
                            pb[:, 0:w], W3[:, kc, hh * 128:(hh + 1) * 128], Hap(kc, c0, w),
                            start=(kc == 0), stop=(kc == KC - 1)),
                            reads=[wg_b[i], H[kc][tt]], writes=[pB])
                    ki = cnt % 2
                    cnt += 1
                    kst = view(OFF_KST + ki * 1024, 512, BF16)
                    rs = view(OFF_RS + ki * 2048, 512, F32)
                    lo = max(tok_lo, c0)
                    hi = c0 + w
                    dcol = dst_lo + (lo - tok_lo)

                    def after(kst=kst, head=head, lo=lo, hi=hi, c0=c0, dcol=dcol, ki=ki):
                        if hi > lo:
                            pr.emit("sp", lambda e: e.dma_start(
                                out=KTd[head, :, dcol:dcol + (hi - lo)], in_=kst[:, lo - c0:hi - c0]),
                                reads=[kst_b[ki]], pwrites=[KTd_b[head]], dkey="kst%d" % ki)
                    tl_ = head_norm(pb, pB, w, 2, kst[:, 0:w], dict(writes=[kst_b[ki]]), (1.0 / DH, EPSC),
                                    rs[:, 0:w], rs_b[ki], after=after)
                    if pend_tail[0] is not None:
                        pend_tail[0]()
                    pend_tail[0] = tl_
        if pend_tail[0] is not None:
            pend_tail[0]()
            pend_tail[0] = None
        nch = (T - tok_lo) // 128
        for vp in range(8):
            i = load_wg_slot(wv3[:, :, vp * 256:(vp + 1) * 256], 256, (KC, 256))
            W3 = wgv(i).rearrange("p (a b) -> p a b", a=KC)
            for ch in range(nch):
                t0 = tok_lo + ch * 128
                tl = tiles_overlapping(t0, 128)
                k = st["gen"]
                st["gen"] += 1
                pb, pB = PS[4 + k % 4], ps_b[4 + k % 4]
                for kc in range(KC):
                    pr.emit("pe", lambda e, pb=pb, W3=W3, kc=kc, t0=t0: e.matmul(
                        pb[:, 0:256], Hap(kc, t0, 128), W3[:, kc, :], start=(kc == 0), stop=(kc == KC - 1)),
                        reads=[wg_b[i]] + [H[kc][x] for x in tl], writes=[pB])
                vi = cnt % 2
                cnt += 1
                vst = view(OFF_VST + vi * 512, 256, BF16)
                pr.emit("act", lambda e, vst=vst, pb=pb: e.activation(out=vst, in_=pb[:, 0:256], func=AF.Copy),
                        reads=[pB], writes=[vst_b[vi]])
                r0 = dst_lo + ch * 128
                pr.emit("sp", lambda e, vst=vst, r0=r0, vp=vp: e.dma_start(
                    out=Vd[r0:r0 + 128, vp * 256:(vp + 1) * 256], in_=vst),
                    reads=[vst_b[vi]], pwrites=[Vd_b], dkey="vst%d" % vi)

    def attn(lb):
        l = 2 + lb
        X, H, tiles, T = st["X"], st["H"], st["tiles"], st["T"]
        assert T == T_OWN
        rmsnorm(V_MIX + l)
        o = OFF_R
        OFF_QT = o; o += 32768
        OFF_KT = o; o += 2 * 3072
        OFF_VH = o; o += 2 * 3072
        OFF_BI = o; o += 2 * 2560
        OFF_TMP = o; o += 3 * 2560
        OFF_PT = o; o += 3 * 1280
        OFF_RD = o; o += 2 * 512
        OFF_RS = o; o += 2 * 2048
        assert o <= TOT
        qt_b = [[pr.sbuf(OFF_QT + (h * 1024 + c0) * 2, w * 2, "qt") for (c0, w) in tiles] for h in range(NH)]
        kt_b = [pr.sbuf(OFF_KT + i * 3072, 3072, "kt") for i in range(2)]
        vh_b = [pr.sbuf(OFF_VH + i * 3072, 3072, "vh") for i in range(2)]
        bi_b = [pr.sbuf(OFF_BI + i * 2560, 2560, "bi") for i in range(2)]
        tmp_b = [pr.sbuf(OFF_TMP + i * 2560, 2560, "tmp") for i in range(3)]
        pt_b = [pr.sbuf(OFF_PT + i * 1280, 1280, "pt") for i in range(3)]
        rd_b = [pr.sbuf(OFF_RD + i * 512, 512, "rd") for i in range(2)]
        rs_b = [pr.sbuf(OFF_RS + i * 2048, 2048, "rs") for i in range(2)]

        def QTap(h, c0, w):
            return view(OFF_QT + (h * 1024 + c0) * 2, w, BF16)

        wq3 = din["w_q"][lb].rearrange("(kc p) d -> p kc d", p=128)
        cnt = 0
        pend_tail = [None]
        for hp in range(8):
            i = load_wg_slot(wq3[:, :, hp * 256:(hp + 1) * 256], 256, (KC, 256))
            W3 = wgv(i).rearrange("p (a b) -> p a b", a=KC)
            for hh in range(2):
                head = hp * 2 + hh
                for tt, (c0, w) in enumerate(tiles):
                    k = st["gen"]
                    st["gen"] += 1
                    pb, pB = PS[4 + k % 4], ps_b[4 + k % 4]
                    for kc in range(KC):
                        pr.emit("pe", lambda e, pb=pb, W3=W3, kc=kc, hh=hh, c0=c0, w=w: e.matmul(
                            pb[:, 0:w], W3[:, kc, hh * 128:(hh + 1) * 128], Hap(kc, c0, w),
                            start=(kc == 0), stop=(kc == KC - 1)),
                            reads=[wg_b[i], H[kc][tt]], writes=[pB])
                    ri = cnt % 2
                    cnt += 1
                    rs = view(OFF_RS + ri * 2048, 512, F32)
                    tl_ = head_norm(pb, pB, w, lb, QTap(head, c0, w), dict(writes=[qt_b[head][tt]]),
                                    (1.0, EPSQ), rs[:, 0:w], rs_b[ri])
                    if pend_tail[0] is not None:
                        pend_tail[0]()
                    pend_tail[0] = tl_
        if pend_tail[0] is not None:
            pend_tail[0]()
            pend_tail[0] = None

        def load_head(h):
            i = h % 2
            ktv = view(OFF_KT + i * 3072, NKEY, BF16)
            vhv = view(OFF_VH + i * 3072, NKEY, BF16).rearrange("p (a b) -> p a b", a=12)
            biv = view(OFF_BI + i * 2560, 640, F32)
            pr.emit("sp", lambda e: e.dma_start(out=ktv, in_=KTd[h]), reads=[KTd_b[h]], writes=[kt_b[i]],
                    dkey="kt%d" % i)
            pr.emit("sp", lambda e: e.dma_start(
                out=vhv, in_=Vd[:, h * 128:(h + 1) * 128].rearrange("(a p) d -> p a d", p=128)),
                reads=[Vd_b], writes=[vh_b[i]], dkey="vh%d" % i)
            pr.emit("sp", lambda e: e.dma_start(out=biv, in_=din["biasG"][lb, h]), writes=[bi_b[i]],
                    dkey="bi%d" % i)
            pr.emit("pool", lambda e: e.tensor_tensor(out=biv, in0=biv, in1=MASKG, op=ALU.add),
                    reads=[bi_b[i], const_b], writes=[bi_b[i]])

        sc = dict(n=0)

        def scores(h, qt):
            i = h % 2
            n = sc["n"]
            sc["n"] += 1
            sa, sA = PS[(n % 3) * 2], ps_b[(n % 3) * 2]
            sb, sB = PS[(n % 3) * 2 + 1], ps_b[(n % 3) * 2 + 1]
            ktv = view(OFF_KT + i * 3072, NKEY, BF16)
            biv = view(OFF_BI + i * 2560, 640, F32)
            tmpv = view(OFF_TMP + (n % 3) * 2560, 640, F32)
            ptv = view(OFF_PT + (n % 3) * 1280, 640, BF16)
            q0 = qt * 128
            tq = q0 // 512
            for kb in range(5):
                dst = sa[:, kb * 128:(kb + 1) * 128] if kb < 4 else sb[:, 0:128]
                dB = sA if kb < 4 else sB
                k0 = (qt + kb) * 128
                pr.emit("pe", lambda e, dst=dst, k0=k0, q0=q0: e.matmul(
                    dst, ktv[:, k0:k0 + 128], QTap(h, q0, 128), start=True, stop=True),
                    reads=[kt_b[i], qt_b[h][tq]], pwrites=[dB])
            nh_ = max(0, 4 - qt)
            segs = []
            if nh_ > 0:
                segs.append((0, min(nh_, 4), HMASK))
            if nh_ < 4:
                segs.append((nh_, 4, ZCOL))
            first = True
            for (a, b, colap) in segs:
                kw = dict(writes=[tmp_b[n % 3]]) if first else dict(pwrites=[tmp_b[n % 3]])
                first = False
                pr.emit("dve", lambda e, a=a, b=b, colap=colap: e.scalar_tensor_tensor(
                    out=tmpv[:, a * 128:b * 128], in0=sa[:, a * 128:b * 128], scalar=colap[:, 0:1],
                    in1=biv[:, a * 128:b * 128], op0=ALU.add, op1=ALU.add),
                    reads=[sA, bi_b[i], const_b], **kw)
            pr.emit("dve", lambda e: e.tensor_tensor(
                out=tmpv[:, 512:640], in0=sb[:, 0:128], in1=biv[:, 512:640], op=ALU.add),
                reads=[sB, bi_b[i]], pwrites=[tmp_b[n % 3]])
            pr.emit("act", lambda e: e.activation(out=ptv, in_=tmpv, func=AF.Exp),
                    reads=[tmp_b[n % 3]], writes=[pt_b[n % 3]])
            return n

        def pv(h, qt, n):
            i = h % 2
            odb, oB = PS[6 + n % 2], ps_b[6 + n % 2]
            vhv = view(OFF_VH + i * 3072, NKEY, BF16).rearrange("p (a b) -> p a b", a=12)
            ptv = view(OFF_PT + (n % 3) * 1280, 640, BF16)
            rdv = view(OFF_RD + (n % 2) * 512, 128, F32)
            q0 = qt * 128
            tq = q0 // 512
            for kb in range(5):
                kw = dict(writes=[oB]) if kb == 0 else dict(pwrites=[oB])
                pr.emit("pe", lambda e, kb=kb: e.matmul(
                    odb[:, 0:128], vhv[:, qt + kb, :], ptv[:, kb * 128:(kb + 1) * 128],
                    start=(kb == 0), stop=(kb == 4)),
                    reads=[vh_b[i], pt_b[n % 3]], **kw)
            for kb in range(5):
                pr.emit("pe", lambda e, kb=kb: e.matmul(
                    odb[:, 128:256], ONES, ptv[:, kb * 128:(kb + 1) * 128],
                    start=(kb == 0), stop=(kb == 4)),
                    reads=[const_b, pt_b[n % 3]], pwrites=[oB])
            pr.emit("act", lambda e: e.activation(out=rdv, in_=odb[:, 128:256], func=AF.Ln),
                    reads=[oB], writes=[rd_b[n % 2]])
            pr.emit("act", lambda e: e.activation(out=rdv, in_=rdv, func=AF.Exp, scale=-1.0),
                    reads=[rd_b[n % 2]], writes=[rd_b[n % 2]])

            def tail():
                pr.emit("dve", lambda e: e.tensor_tensor(
                    out=Hap(h, q0, 128), in0=odb[:, 0:128], in1=rdv, op=ALU.mult),
                    reads=[oB, rd_b[n % 2]], pwrites=[H[h][tq]])
            return tail

        load_head(0)
        hist = []
        tails = []
        for h in range(NH):
            for qt in range(8):
                n = scores(h, qt)
                hist.append((h, qt, n))
                if tails:
                    tails.pop(0)()
                if len(hist) > 2:
                    tails.append(pv(*hist.pop(0)))
                if qt == 1 and h + 1 < NH:
                    load_head(h + 1)
        while hist or tails:
            if tails:
                tails.pop(0)()
            if hist:
                tails.append(pv(*hist.pop(0)))

        wo3 = din["w_o"][lb].rearrange("(kc p) d -> p kc d", p=128)
        for dp in range(8):
            i = load_wg_slot(wo3[:, :, dp * 256:(dp + 1) * 256], 256, (KC, 256))
            W3 = wgv(i).rearrange("p (a b) -> p a b", a=KC)
            for dd in range(2):
                dc = dp * 2 + dd
                for tt, (c0, w) in enumerate(tiles):
                    k = st["gen"]
                    st["gen"] += 1
                    pb, pB = PS[4 + k % 4], ps_b[4 + k % 4]
                    for kc in range(KC):
                        pr.emit("pe", lambda e, pb=pb, W3=W3, kc=kc, dd=dd, c0=c0, w=w: e.matmul(
                            pb[:, 0:w], W3[:, kc, dd * 128:(dd + 1) * 128], Hap(kc, c0, w),
                            start=(kc == 0), stop=(kc == KC - 1)),
                            reads=[wg_b[i], H[kc][tt]], writes=[pB])
                    pr.emit("dve", lambda e, pb=pb, dc=dc, c0=c0, w=w: e.tensor_tensor(
                        out=Xap(dc, c0, w), in0=pb[:, 0:w], in1=Xap(dc, c0, w), op=ALU.add),
                        reads=[pB, X[dc][tt]], writes=[X[dc][tt]])

    setup()
    own_tiles = [(0, 512), (512, 512)]
    halo_tiles = [(0, 272), (272, 272)]
    for pi, stp in enumerate(plan):
        kind = stp[0]
        if kind == "load_own":
            load_x(din["xT"], T_OWN, own_tiles)
        elif kind == "load_halo":
            load_x(din["xhT"], T_HALO, halo_tiles)
        elif kind == "retile":
            set_pass(st["T"], list(stp[1]))
        elif kind == "ffn":
            ffn(stp[1], stp[2])
        elif kind == "pool":
            pool(stp[1], stp[2])
        elif kind == "kv_halo":
            nxt = plan[pi + 1][0] if pi + 1 < len(plan) else None
            if nxt == "load_own":
                kv(T_HALO - HALO_KV, 0,
                   after_norm=lambda: st.__setitem__("xpre", prefetch_x(din["xT"], T_OWN, own_tiles)))
            else:
                kv(T_HALO - HALO_KV, 0)
        elif kind == "kv_own":
            kv(0, HALO_KV)
        elif kind == "attn":
            attn(stp[1])
        elif kind == "store":
            store_x()
        else:
            raise ValueError(kind)
    pr.lower(nc)
    nc._in_names = list(din.keys())
    nc._n_ops = {e: len(pr.ops[e]) for e in ENGS}
    return nc


_TA = ((0, 272), (272, 272))
_TB = ((16, 264), (280, 264))
_TC = ((32, 256), (288, 256))
FULL_PLAN = (
    [("load_halo",), ("ffn", 0, 1), ("pool", 0, True), ("retile", _TB), ("ffn", 0, 2), ("ffn", 1, 1),
     ("retile", _TA), ("pool", 1, True), ("retile", _TC), ("ffn", 1, 2), ("kv_halo",), ("load_own",)]
    + [s for l in (0, 1) for s in (("ffn", l, 1), ("pool", l, False), ("ffn", l, 2))]
    + [("kv_own",)]
    + [s for lb in (0, 1) for s in (("ffn", 2 + lb, 1), ("attn", lb), ("ffn", 2 + lb, 2))]
    + [("store",)]
)


def _vec_layout(v):
    return np.ascontiguousarray(np.asarray(v, np.float32).reshape(KC, 128).T)


def host_prep(inputs):
    x = np.asarray(inputs["x"], np.float32)[0]
    vecs = np.zeros((128, NVEC * KC), np.float32)

    def put(idx, v):
        vecs[:, idx * KC:(idx + 1) * KC] = _vec_layout(v)
    for l in range(4):
        put(V_FFN1 + l, inputs["ffn1_norm"][l])
        put(V_MIX + l, inputs["mix_norm"][l])
        put(V_FFN2 + l, inputs["ffn2_norm"][l])
    put(V_KV, inputs["kv_norm"])
    for l in range(2):
        put(V_PSC + l, inputs["pool_scale"][l])
    hg = np.zeros((128, 4), np.float32)
    hg[:, 0] = np.asarray(inputs["q_gain"], np.float32)[0]
    hg[:, 1] = np.asarray(inputs["q_gain"], np.float32)[1]
    hg[:, 2] = np.asarray(inputs["k_gain"], np.float32)
    m = np.arange(128)[:, None, None]
    kb = np.arange(5)[None, :, None]
    r = np.arange(128)[None, None, :]
    kk = kb * 128 + m
    rel = (512 + r) - kk
    idx = np.clip(rel, -63, 128) + 63
    qc = 8 + r // 64
    kc_ = kk // 64
    valid = (qc - kc_ >= 0) & (qc - kc_ <= 8)
    maskG = np.where(valid, 0.0, NEG).astype(np.float32).reshape(128, 640)
    rb = np.asarray(inputs["rel_bias"], np.float32)
    biasG = np.ascontiguousarray(rb[:, :, idx].reshape(2, NH, 128, 640))
    xT = np.ascontiguousarray(x.T)
    shared = {"vecs": vecs, "hg": hg, "maskG": maskG, "biasG": biasG}
    for nm in ("ffn1_w_gate", "ffn1_w_up", "ffn1_w_down", "ffn2_w_gate", "ffn2_w_up", "ffn2_w_down",
               "pool_w", "w_k", "w_v", "w_q", "w_o"):
        shared[nm] = np.ascontiguousarray(np.asarray(inputs[nm], np.float32))
    maps = []
    for c in range(NCORES):
        t0 = c * T_OWN
        d = dict(shared)
        d["xT"] = np.ascontiguousarray(xT[:, t0:t0 + T_OWN])
        xh = np.zeros((D, T_HALO), np.float32)
        lo = t0 - T_HALO
        if lo >= 0:
            xh[:, :] = xT[:, lo:t0]
        elif t0 > 0:
            xh[:, -t0:] = xT[:, 0:t0]
        d["xhT"] = xh
        d["hmask"] = np.full((128, 1), 0.0 if c > 0 else NEG, np.float32)
        invc = np.zeros((128, 64), np.float32)
        for g, wdw in enumerate(POOL_W):
            for t in range(16):
                cntv = min(t0 + t + 1, wdw)
                invc[:, g * 16 + t] = 1.0 / cntv
        d["invc"] = invc
        maps.append(d)
    return maps


_NC_CACHE = {}


def kernel(**inputs):
    maps = host_prep(inputs)
    if "full" not in _NC_CACHE:
        _NC_CACHE["full"] = build(FULL_PLAN)
    nc = _NC_CACHE["full"]
    maps = [{k: m[k] for k in nc._in_names} for m in maps]
    res = run_bass_kernel_spmd(nc, maps, core_ids=list(range(NCORES)))
    outs = [np.asarray(r["outT"], np.float32) for r in res.results]
    full = np.concatenate(outs, axis=1)
    return np.ascontiguousarray(full.T)[None].astype(np.float32)
```

```python
import numpy as np
import concourse.bass as bass
import concourse.mybir as mybir
from concourse.bass_utils import run_bass_kernel_spmd

F32 = mybir.dt.float32
BF16 = mybir.dt.bfloat16
AF = mybir.ActivationFunctionType
ALU = mybir.AluOpType

NCORES = 8
D = 2048
KC = 16
DFF = 5504
NFC = 43
NH = 16
DH = 128
SEQ = 8192
T_OWN = 1024
HALO_KV = 512
T_HALO = 544
NKEY = T_OWN + HALO_KV
EPS = 1e-6
NEG = -30000.0
POOL_W = (2, 4, 8, 16)
NS = 3
FG = 2

V_FFN1 = 0
V_MIX = 4
V_FFN2 = 8
V_KV = 12
V_PSC = 13
NVEC = 15

ENGS = ("pe", "act", "dve", "pool", "sp")


class Op:
    __slots__ = ("eng", "fn", "deps", "idx", "dkey", "val", "inc")


class Buf:
    __slots__ = ("w", "r", "off", "size", "name")

    def __init__(self, name=""):
        self.w = {}
        self.r = {}
        self.off = None
        self.size = 0
        self.name = name


class Prog:
    def __init__(self):
        self.ops = {e: [] for e in ENGS}
        self.live = []
        self.final = []

    def sbuf(self, off, size, name=""):
        b = Buf(name)
        b.off, b.size = off, size
        keep = []
        for o in self.live:
            if o.off < off + size and off < o.off + o.size:
                for k, op in o.w.items():
                    b.r[("a", id(op))] = op
                for k, op in o.r.items():
                    b.r[("a", id(op))] = op
            else:
                keep.append(o)
        keep.append(b)
        self.live = keep
        return b

    def emit(self, eng, fn, reads=(), writes=(), pwrites=(), dkey=None):
        op = Op()
        op.eng, op.fn, op.dkey = eng, fn, dkey
        op.inc = dkey is not None
        op.val = 0
        op.idx = len(self.ops[eng])
        deps = {}
        for b in reads:
            for o in b.w.values():
                deps[id(o)] = o
        for b in writes:
            for o in b.w.values():
                deps[id(o)] = o
            for o in b.r.values():
                deps[id(o)] = o
        for b in pwrites:
            for o in b.r.values():
                deps[id(o)] = o
        op.deps = list(deps.values())
        key = dkey if dkey else eng
        for b in reads:
            if b not in writes:
                b.r[key] = op
        for b in writes:
            b.w = {key: op}
            b.r = {}
        for b in pwrites:
            b.w[key] = op
        self.ops[eng].append(op)
        return op

    @staticmethod
    def _needs_wait(op, d):
        if d.dkey is None and op.dkey is None and d.eng == op.eng:
            if op.eng == "pe":
                return False
            return (op.idx - d.idx) <= 2
        return True

    def lower(self, nc):
        for e in ENGS:
            for op in self.ops[e]:
                for d in op.deps:
                    if self._needs_wait(op, d):
                        d.inc = True
        dcount = {}
        for e in ENGS:
            c = 0
            for op in self.ops[e]:
                if op.dkey is None:
                    if op.inc:
                        c += 1
                    op.val = c
                else:
                    dcount[op.dkey] = dcount.get(op.dkey, 0) + 16
                    op.val = dcount[op.dkey]
        esem = {e: nc.alloc_semaphore("s_" + e) for e in ENGS}
        dsem = {k: nc.alloc_semaphore("d_" + k) for k in dcount}
        finals = self.final

        def run(e, eng):
            waited = {}
            for op in self.ops[e]:
                for d in op.deps:
                    if not self._needs_wait(op, d):
                        continue
                    key = ("d", d.dkey) if d.dkey else ("e", d.eng)
                    if waited.get(key, 0) >= d.val:
                        continue
                    sem = dsem[d.dkey] if d.dkey else esem[d.eng]
                    eng.wait_ge(sem, d.val)
                    waited[key] = d.val
                ins = op.fn(eng)
                if op.dkey:
                    ins.then_inc(dsem[op.dkey], 16)
                elif op.inc:
                    ins.then_inc(esem[e], 1)
            if e == "sp":
                for op in finals:
                    key = ("d", op.dkey)
                    if waited.get(key, 0) < op.val:
                        eng.wait_ge(dsem[op.dkey], op.val)
                        waited[key] = op.val

        with nc.Block() as block:
            @block.tensor
            def _(eng):
                run("pe", eng)

            @block.scalar
            def _(eng):
                run("act", eng)

            @block.vector
            def _(eng):
                run("dve", eng)

            @block.gpsimd
            def _(eng):
                run("pool", eng)

            @block.sync
            def _(eng):
                run("sp", eng)


def build(plan, kv_kind="Internal"):
    nc = bass.Bass("TRN2", target_bir_lowering=False)
    pr = Prog()
    SHAPES = {"xT": [D, T_OWN], "xhT": [D, T_HALO], "vecs": [128, NVEC * KC], "hg": [128, 4],
              "hmask": [128, 1], "invc": [128, 64], "maskG": [128, 640], "biasG": [2, NH, 128, 640],
              "ffn1_w_gate": [4, D, DFF], "ffn1_w_up": [4, D, DFF], "ffn2_w_gate": [4, D, DFF],
              "ffn2_w_up": [4, D, DFF], "ffn1_w_down": [4, DFF, D], "ffn2_w_down": [4, DFF, D],
              "pool_w": [2, 4, 512, 512], "w_k": [D, D], "w_v": [D, D], "w_q": [2, D, D], "w_o": [2, D, D]}

    class _Lazy(dict):
        def __missing__(self, name):
            self[name] = nc.dram_tensor(name, list(SHAPES[name]), F32, kind="ExternalInput").ap()
            return self[name]
    din = _Lazy()
    outT = nc.dram_tensor("outT", [D, T_OWN], F32, kind="ExternalOutput").ap()
    kvk = kv_kind
    KTd = nc.dram_tensor("KTd", [NH, 128, NKEY], BF16, kind=kvk).ap()
    Vd = nc.dram_tensor("Vd", [NKEY, D], BF16, kind=kvk).ap()
    KTd_b = [Buf("KTd%d" % h) for h in range(NH)]
    Vd_b = Buf("Vd")

    TOT = 212000
    big = nc.alloc_sbuf_tensor("big", [128, TOT // 4], F32)

    def view(off, n, dt):
        assert off % 4 == 0
        if dt == F32:
            return big[:, off // 4: off // 4 + n]
        nb = n * 2
        assert nb % 4 == 0
        return big[:, off // 4: off // 4 + nb // 4].bitcast(BF16)

    OFF_X = 0
    OFF_H = 65536
    OFF_C = 98304
    o = OFF_C
    OFF_VEC = o; o += NVEC * KC * 4
    OFF_HG = o; o += 16
    OFF_HMASK = o; o += 4
    OFF_ZCOL = o; o += 4
    OFF_EPSC = o; o += 4
    OFF_EPSQ = o; o += 4
    OFF_INVC = o; o += 256
    OFF_MASKG = o; o += 2560
    OFF_ONES = o; o += 256
    OFF_PHALO = o; o += 2 * KC * 16 * 4
    OFF_ZERO16 = o; o += 64
    OFF_RSTD = o; o += 4096
    OFF_SQ = o; o += 3 * 1024
    OFF_WG = o; o += NS * 8192
    OFF_R = o
    assert OFF_R + 61440 <= TOT, OFF_R

    VEC = view(OFF_VEC, NVEC * KC, F32)
    HG = view(OFF_HG, 4, F32)
    HMASK = view(OFF_HMASK, 1, F32)
    ZCOL = view(OFF_ZCOL, 1, F32)
    EPSC = view(OFF_EPSC, 1, F32)
    EPSQ = view(OFF_EPSQ, 1, F32)
    INVC = view(OFF_INVC, 64, F32)
    MASKG = view(OFF_MASKG, 640, F32)
    ONES = view(OFF_ONES, 128, BF16)
    PHALO = view(OFF_PHALO, 2 * KC * 16, F32)
    ZERO16 = view(OFF_ZERO16, 16, F32)
    RSTD = view(OFF_RSTD, 1024, F32)
    const_b = pr.sbuf(OFF_VEC, OFF_PHALO - OFF_VEC, "const")
    phalo_b = [[pr.sbuf(OFF_PHALO + (l * KC + kc) * 64, 64, "ph") for kc in range(KC)] for l in range(2)]
    zero16_b = pr.sbuf(OFF_ZERO16, 64, "z16")
    SQ = [view(OFF_SQ + i * 1024, 512, BF16) for i in range(3)]
    sq_b = [pr.sbuf(OFF_SQ + i * 1024, 1024, "sq") for i in range(3)]
    wg_b = [pr.sbuf(OFF_WG + i * 8192, 8192, "wg%d" % i) for i in range(NS)]

    def wgv(i):
        return view(OFF_WG + i * 8192, 4096, BF16)

    PS = [nc.alloc_psum_tensor("ps%d" % i, [128, 512], F32) for i in range(8)]
    ps_b = [Buf("ps%d" % i) for i in range(8)]

    st = dict(sq=0, wg=0, gen=0, ss=0, T=None, tiles=None, X=None, H=None, rstd=None)

    def Xap(kc, c0, w):
        return view(OFF_X + (kc * 1024 + c0) * 4, w, F32)

    def Hap(kc, c0, w):
        return view(OFF_H + (kc * 1024 + c0) * 2, w, BF16)

    def mk_xbufs(tiles):
        return [[pr.sbuf(OFF_X + (kc * 1024 + c0) * 4, w * 4, "x") for (c0, w) in tiles] for kc in range(KC)]

    def set_pass(T, tiles, xbufs=None):
        st["T"], st["tiles"] = T, tiles
        st["X"] = xbufs if xbufs is not None else mk_xbufs(tiles)
        st["H"] = [[pr.sbuf(OFF_H + (kc * 1024 + c0) * 2, w * 2, "h") for (c0, w) in tiles] for kc in range(KC)]
        st["rstd"] = [pr.sbuf(OFF_RSTD + c0 * 4, w * 4, "rstd") for (c0, w) in tiles]

    def tiles_overlapping(c0, w):
        return [i for i, (a, b) in enumerate(st["tiles"]) if a < c0 + w and c0 < a + b]

    def setup():
        pr.emit("sp", lambda e: e.dma_start(out=VEC, in_=din["vecs"]), writes=[const_b], dkey="c0")
        pr.emit("sp", lambda e: e.dma_start(out=HG, in_=din["hg"]), pwrites=[const_b], dkey="c1")
        pr.emit("sp", lambda e: e.dma_start(out=HMASK, in_=din["hmask"]), pwrites=[const_b], dkey="c2")
        pr.emit("sp", lambda e: e.dma_start(out=INVC, in_=din["invc"]), pwrites=[const_b], dkey="c3")
        pr.emit("sp", lambda e: e.dma_start(out=MASKG, in_=din["maskG"]), pwrites=[const_b], dkey="c4")
        pr.emit("dve", lambda e: e.memset(ONES, 1.0), pwrites=[const_b])
        pr.emit("dve", lambda e: e.memset(ZCOL, 0.0), pwrites=[const_b])
        pr.emit("dve", lambda e: e.memset(EPSC, EPS), pwrites=[const_b])
        pr.emit("dve", lambda e: e.memset(EPSQ, EPS * DH), pwrites=[const_b])
        pr.emit("dve", lambda e: e.memset(ZERO16, 0.0), writes=[zero16_b])

    def prefetch_x(src, T, tiles):
        xb = mk_xbufs(tiles)
        s3 = src.rearrange("(kc p) t -> p kc t", p=128)
        for tt, (c0, w) in enumerate(tiles):
            for q in range(4):
                dst = view(OFF_X + q * 4 * 4096, 4 * 1024, F32).rearrange("p (k t) -> p k t", k=4)[:, :, c0:c0 + w]
                bl = [xb[kc][tt] for kc in range(q * 4, q * 4 + 4)]
                pr.emit("sp", lambda e, dst=dst, q=q, c0=c0, w=w: e.dma_start(
                    out=dst, in_=s3[:, q * 4:(q + 1) * 4, c0:c0 + w]),
                    writes=bl, dkey="xin%d" % (tt * 4 + q))
        return xb

    def load_x(src, T, tiles):
        xb = st.pop("xpre", None)
        if xb is None:
            xb = prefetch_x(src, T, tiles)
        set_pass(T, tiles, xb)

    def store_x():
        T, tiles = st["T"], st["tiles"]
        d3 = outT.rearrange("(kc p) t -> p kc t", p=128)
        for tt, (c0, w) in enumerate(tiles):
            for q in range(4):
                srcv = view(OFF_X + q * 4 * 4096, 4 * 1024, F32).rearrange("p (k t) -> p k t", k=4)[:, :, c0:c0 + w]
                bl = [st["X"][kc][tt] for kc in range(q * 4, q * 4 + 4)]
                op = pr.emit("sp", lambda e, srcv=srcv, q=q, c0=c0, w=w: e.dma_start(
                    out=d3[:, q * 4:(q + 1) * 4, c0:c0 + w], in_=srcv),
                    reads=bl, dkey="xout%d" % (tt * 4 + q))
                pr.final.append(op)

    def rmsnorm(vidx, write_h=True):
        X, H = st["X"], st["H"]
        for tt, (c0, w) in enumerate(st["tiles"]):
            ssi = st["ss"] % 2
            st["ss"] += 1
            ssb, ssB = PS[ssi], ps_b[ssi]
            for kc in range(KC):
                si = st["sq"] % 3
                st["sq"] += 1
                pr.emit("act", lambda e, si=si, kc=kc, c0=c0, w=w: e.activation(
                    out=SQ[si][:, 0:w], in_=Xap(kc, c0, w), func=AF.Square),
                    reads=[X[kc][tt]], writes=[sq_b[si]])
                pr.emit("pe", lambda e, si=si, kc=kc, w=w, ssb=ssb: e.matmul(
                    ssb[:, 0:w], ONES, SQ[si][:, 0:w], start=(kc == 0), stop=(kc == KC - 1)),
                    reads=[sq_b[si], const_b], writes=[ssB])
            rs = RSTD[:, c0:c0 + w]
            pr.emit("act", lambda e, rs=rs, ssb=ssb, w=w: e.activation(
                out=rs, in_=ssb[:, 0:w], func=AF.Ln, bias=EPSC[:, 0:1], scale=1.0 / D),
                reads=[ssB, const_b], writes=[st["rstd"][tt]])
            pr.emit("act", lambda e, rs=rs: e.activation(out=rs, in_=rs, func=AF.Exp, scale=-0.5),
                    reads=[st["rstd"][tt]], writes=[st["rstd"][tt]])
            for kc in range(KC if write_h else 0):
                col = vidx * KC + kc
                pr.emit("dve", lambda e, kc=kc, c0=c0, w=w, col=col, rs=rs: e.scalar_tensor_tensor(
                    out=Hap(kc, c0, w), in0=Xap(kc, c0, w), scalar=VEC[:, col:col + 1], in1=rs,
                    op0=ALU.mult, op1=ALU.mult),
                    reads=[X[kc][tt], st["rstd"][tt], const_b], writes=[H[kc][tt]])

    def load_wg_slot(src3, ncols_total, shape3):
        i = st["wg"] % NS
        st["wg"] += 1
        a, b = shape3
        dst = wgv(i)[:, 0:a * b].rearrange("p (a b) -> p a b", a=a)
        pr.emit("pool", lambda e, dst=dst, src3=src3: e.dma_start(out=dst, in_=src3),
                writes=[wg_b[i]], dkey="wg%d" % i)
        return i

    def ffn(l, which):
        X, H, tiles = st["X"], st["H"], st["tiles"]
        nt = len(tiles)
        pre = "ffn%d_" % which
        wg3 = din[pre + "w_gate"][l].rearrange("(kc p) f -> p kc f", p=128)
        wu3 = din[pre + "w_up"][l].rearrange("(kc p) f -> p kc f", p=128)
        wd3 = din[pre + "w_down"][l].rearrange("(fc p) d -> p fc d", p=128)
        rmsnorm((V_FFN1 if which == 1 else V_FFN2) + l)
        OFF_WU = OFF_R
        OFF_WD = OFF_WU + NS * 8192
        OFF_ACT = OFF_WD + NS * 8192
        OFF_SG = OFF_ACT + 2 * 4096
        wu_b = [pr.sbuf(OFF_WU + i * 8192, 8192, "wu") for i in range(NS)]
        wd_b = [pr.sbuf(OFF_WD + i * 8192, 8192, "wd") for i in range(NS)]
        act_b = [[[pr.sbuf(OFF_ACT + s * 4096 + (j * 1024 + c0) * 2, w * 2, "act") for (c0, w) in tiles]
                  for j in range(FG)] for s in range(2)]
        sg_b = [pr.sbuf(OFF_SG + i * 2048, 2048, "sg") for i in range(2)]

        def WU3(i):
            return view(OFF_WU + i * 8192, 4096, BF16).rearrange("p (a b) -> p a b", a=KC)

        def WG3(i):
            return wgv(i).rearrange("p (a b) -> p a b", a=KC)

        def WD3(i):
            return view(OFF_WD + i * 8192, 4096, BF16).rearrange("p (a b) -> p a b", a=FG)

        def ACTap(s, j, c0, w):
            return view(OFF_ACT + s * 4096 + (j * 1024 + c0) * 2, w, BF16)

        def SGap(i, w):
            return view(OFF_SG + i * 2048, w, F32)

        groups = []
        f = 0
        while f < NFC:
            n = min(FG, NFC - f)
            groups.append((f, n))
            f += n
        slots = {}
        cnt = dict(gu=0, d=0, ring=0)

        def load_group(g):
            f0, n = groups[g]
            gi = st["wg"] % NS
            st["wg"] += 1
            ri = cnt["ring"] % NS
            cnt["ring"] += 1
            fw = n * 128
            pr.emit("pool", lambda e, gi=gi, f0=f0, fw=fw: e.dma_start(
                out=WG3(gi)[:, :, 0:fw], in_=wg3[:, :, f0 * 128:f0 * 128 + fw]),
                writes=[wg_b[gi]], dkey="wg%d" % gi)
            pr.emit("pool", lambda e, ri=ri, f0=f0, fw=fw: e.dma_start(
                out=WU3(ri)[:, :, 0:fw], in_=wu3[:, :, f0 * 128:f0 * 128 + fw]),
                writes=[wu_b[ri]], dkey="wu%d" % ri)
            pr.emit("pool", lambda e, ri=ri, f0=f0, n=n: e.dma_start(
                out=WD3(ri)[:, 0:n, :], in_=wd3[:, f0:f0 + n, :]),
                writes=[wd_b[ri]], dkey="wd%d" % ri)
            slots[g] = (gi, ri)

        def gu_tile(g, j, tt, dl=()):
            gi, ri = slots[g]
            c0, w = tiles[tt]
            k = cnt["gu"]
            cnt["gu"] += 1
            gb, gB = PS[k % 2], ps_b[k % 2]
            ub, uB = PS[2 + k % 2], ps_b[2 + k % 2]
            dl = list(dl)
            nseg = 4
            per = (len(dl) + nseg - 1) // nseg
            segs = [dl[i * per:(i + 1) * per] for i in range(nseg)]

            def gate(kc):
                pr.emit("pe", lambda e, kc=kc: e.matmul(
                    gb[:, 0:w], WG3(gi)[:, kc, j * 128:(j + 1) * 128], Hap(kc, c0, w),
                    start=(kc == 0), stop=(kc == KC - 1)),
                    reads=[wg_b[gi], H[kc][tt]], writes=[gB])

            def up(kc):
                pr.emit("pe", lambda e, kc=kc: e.matmul(
                    ub[:, 0:w], WU3(ri)[:, kc, j * 128:(j + 1) * 128], Hap(kc, c0, w),
                    start=(kc == 0), stop=(kc == KC - 1)),
                    reads=[wu_b[ri], H[kc][tt]], writes=[uB])
            for kc in range(0, 8):
                gate(kc)
            for a in segs[0]:
                d_tile(*a)
            for kc in range(8, KC):
                gate(kc)
            pr.emit("act", lambda e: e.activation(
                out=SGap(k % 2, w), in_=gb[:, 0:w], func=AF.Silu),
                reads=[gB], writes=[sg_b[k % 2]])
            for a in segs[1]:
                d_tile(*a)
            for kc in range(0, 8):
                up(kc)
            for a in segs[2]:
                d_tile(*a)
            for kc in range(8, KC):
                up(kc)
            pr.emit("dve", lambda e: e.tensor_tensor(
                out=ACTap(g % 2, j, c0, w), in0=SGap(k % 2, w), in1=ub[:, 0:w], op=ALU.mult),
                reads=[sg_b[k % 2], uB], writes=[act_b[g % 2][j][tt]])
            for a in segs[3]:
                d_tile(*a)

        def d_tile(g, dc, tt):
            gi, ri = slots[g]
            f0, n = groups[g]
            c0, w = tiles[tt]
            k = cnt["d"]
            cnt["d"] += 1
            db, dB = PS[4 + k % 4], ps_b[4 + k % 4]
            for j in range(n):
                pr.emit("pe", lambda e, j=j, db=db, ri=ri, dc=dc, g=g, c0=c0, w=w, n=n: e.matmul(
                    db[:, 0:w], WD3(ri)[:, j, dc * 128:(dc + 1) * 128], ACTap(g % 2, j, c0, w),
                    start=(j == 0), stop=(j == n - 1)),
                    reads=[wd_b[ri], act_b[g % 2][j][tt]], writes=[dB])
            pr.emit("dve", lambda e, db=db, dc=dc, c0=c0, w=w: e.scalar_tensor_tensor(
                out=Xap(dc, c0, w), in0=db[:, 0:w], scalar=0.5, in1=Xap(dc, c0, w),
                op0=ALU.mult, op1=ALU.add),
                reads=[dB, X[dc][tt]], writes=[X[dc][tt]])

        ng = len(groups)
        for g in range(ng + 1):
            gu_list = []
            d_list = []
            if g < ng:
                load_group(g)
                gu_list = [(g, j, tt) for j in range(groups[g][1]) for tt in range(nt)]
            if g >= 1:
                d_list = [(g - 1, dc, tt) for tt in range(nt) for dc in range(KC)]
            if not gu_list:
                for a in d_list:
                    d_tile(*a)
                continue
            per = (len(d_list) + len(gu_list) - 1) // len(gu_list) if d_list else 0
            for ii, a in enumerate(gu_list):
                gu_tile(*a, dl=d_list[ii * per:(ii + 1) * per])

    def pool(l, halo_mode):
        X, H, tiles, T = st["X"], st["H"], st["tiles"], st["T"]
        nt = len(tiles)
        rmsnorm(V_MIX + l, write_h=False)
        LW = (16 + 1024) * 4
        OFF_HC = OFF_R
        OFF_SA = OFF_HC + 2 * LW
        OFF_T16 = OFF_SA + 2 * LW
        hc_b = [pr.sbuf(OFF_HC + i * LW, LW, "hc") for i in range(2)]
        sa_b = [pr.sbuf(OFF_SA + i * LW, LW, "sa") for i in range(2)]
        t16_b = pr.sbuf(OFF_T16, 64, "t16")
        T16 = view(OFF_T16, 16, F32)

        def HC(i):
            return view(OFF_HC + i * LW, 16 + 1024, F32)

        def SA(i):
            return view(OFF_SA + i * LW, 16 + 1024, F32)

        W = 16 + T
        for dc in range(KC):
            g = dc // 4
            win = POOL_W[g]
            hi = dc % 2
            hc, hcB = HC(hi), hc_b[hi]
            col = (V_MIX + l) * KC + dc
            xall = view(OFF_X + dc * 4096, T, F32)
            pr.emit("dve", lambda e, hc=hc, xall=xall, col=col, T=T: e.scalar_tensor_tensor(
                out=hc[:, 16:16 + T], in0=xall, scalar=VEC[:, col:col + 1], in1=RSTD[:, 0:T],
                op0=ALU.mult, op1=ALU.mult),
                reads=[X[dc][i] for i in range(nt)] + st["rstd"] + [const_b], pwrites=[hcB])
            if halo_mode:
                pr.emit("act", lambda e, hc=hc: e.activation(out=hc[:, 0:16], in_=ZERO16, func=AF.Copy),
                        reads=[zero16_b], pwrites=[hcB])
            else:
                ph = PHALO[:, (l * KC + dc) * 16:(l * KC + dc + 1) * 16]
                pr.emit("act", lambda e, hc=hc, ph=ph: e.activation(out=hc[:, 0:16], in_=ph, func=AF.Copy),
                        reads=[phalo_b[l][dc]], pwrites=[hcB])
            src, srcB = hc, hcB
            k = 1
            si = 0
            v0 = 0
            while k < win:
                dst, dstB = SA(si), sa_b[si]
                v1 = v0 + k
                pr.emit("dve", lambda e, dst=dst, src=src, k=k, W=W, v1=v1: e.tensor_tensor(
                    out=dst[:, v1:W], in0=src[:, v1:W], in1=src[:, v1 - k:W - k], op=ALU.add),
                    reads=[srcB], writes=[dstB])
                src, srcB = dst, dstB
                si ^= 1
                v0 = v1
                k *= 2
            pr.emit("dve", lambda e, src=src, hc=hc, dc=dc, T=T, win=win: e.scalar_tensor_tensor(
                out=Hap(dc, 0, T), in0=src[:, 16:16 + T], scalar=1.0 / win, in1=hc[:, 16:16 + T],
                op0=ALU.mult, op1=ALU.subtract),
                reads=[srcB, hcB], writes=[H[dc][i] for i in range(nt)])
            pr.emit("dve", lambda e, src=src, g=g: e.tensor_tensor(
                out=T16, in0=src[:, 16:32], in1=INVC[:, g * 16:(g + 1) * 16], op=ALU.mult),
                reads=[srcB, const_b], writes=[t16_b])
            pr.emit("dve", lambda e, hc=hc, dc=dc: e.tensor_tensor(
                out=Hap(dc, 0, 16), in0=T16, in1=hc[:, 16:32], op=ALU.subtract),
                reads=[t16_b, hcB], pwrites=[H[dc][0]])
            if halo_mode:
                ph = PHALO[:, (l * KC + dc) * 16:(l * KC + dc + 1) * 16]
                pr.emit("act", lambda e, hc=hc, ph=ph, T=T: e.activation(out=ph, in_=hc[:, T:T + 16], func=AF.Copy),
                        reads=[hcB], writes=[phalo_b[l][dc]])
        for g in range(4):
            src3 = din["pool_w"][l, g].rearrange("(kc p) d -> p kc d", p=128)
            i = load_wg_slot(src3, 512, (4, 512))
            W3 = wgv(i)[:, 0:2048].rearrange("p (a b) -> p a b", a=4)
            for dco in range(4):
                dc = g * 4 + dco
                col = (V_PSC + l) * KC + dc
                for tt, (c0, w) in enumerate(tiles):
                    k = st["gen"]
                    st["gen"] += 1
                    pb, pB = PS[4 + k % 4], ps_b[4 + k % 4]
                    for kc in range(4):
                        pr.emit("pe", lambda e, pb=pb, W3=W3, kc=kc, dco=dco, g=g, c0=c0, w=w: e.matmul(
                            pb[:, 0:w], W3[:, kc, dco * 128:(dco + 1) * 128], Hap(g * 4 + kc, c0, w),
                            start=(kc == 0), stop=(kc == 3)),
                            reads=[wg_b[i], H[g * 4 + kc][tt]], writes=[pB])
                    pr.emit("dve", lambda e, pb=pb, dc=dc, c0=c0, w=w, col=col: e.scalar_tensor_tensor(
                        out=Xap(dc, c0, w), in0=pb[:, 0:w], scalar=VEC[:, col:col + 1], in1=Xap(dc, c0, w),
                        op0=ALU.mult, op1=ALU.add),
                        reads=[pB, X[dc][tt], const_b], writes=[X[dc][tt]])

    def head_norm(pb, pB, w, gcol, out_ap, out_bufs, post_scale, rs_ap, rs_b, after=None):
        si = st["sq"] % 3
        st["sq"] += 1
        pr.emit("act", lambda e: e.activation(out=SQ[si][:, 0:w], in_=pb[:, 0:w], func=AF.Square),
                reads=[pB], writes=[sq_b[si]])

        def tail():
            ssi = st["ss"] % 2
            st["ss"] += 1
            ssb, ssB = PS[ssi], ps_b[ssi]
            pr.emit("pe", lambda e: e.matmul(ssb[:, 0:w], ONES, SQ[si][:, 0:w], start=True, stop=True),
                    reads=[sq_b[si], const_b], writes=[ssB])
            pr.emit("act", lambda e: e.activation(
                out=rs_ap, in_=ssb[:, 0:w], func=AF.Ln, bias=post_scale[1][:, 0:1], scale=post_scale[0]),
                reads=[ssB, const_b], writes=[rs_b])
            pr.emit("act", lambda e: e.activation(out=rs_ap, in_=rs_ap, func=AF.Exp, scale=-0.5),
                    reads=[rs_b], writes=[rs_b])
            pr.emit("dve", lambda e: e.scalar_tensor_tensor(
                out=out_ap, in0=pb[:, 0:w], scalar=HG[:, gcol:gcol + 1], in1=rs_ap, op0=ALU.mult, op1=ALU.mult),
                reads=[pB, rs_b, const_b], **out_bufs)
            if after is not None:
                after()
        return tail

    def kv(tok_lo, dst_lo, after_norm=None):
        X, H, tiles, T = st["X"], st["H"], st["tiles"], st["T"]
        rmsnorm(V_KV)
        if after_norm is not None:
            after_norm()
        OFF_KST = OFF_R
        NST = 8
        OFF_VST = OFF_KST + NST * 1024
        OFF_RS = OFF_VST + NST * 512
        kst_b = [pr.sbuf(OFF_KST + i * 1024, 1024, "kst") for i in range(NST)]
        vst_b = [pr.sbuf(OFF_VST + i * 512, 512, "vst") for i in range(NST)]
        rs_b = [pr.sbuf(OFF_RS + i * 2048, 2048, "rs") for i in range(2)]
        cnt = 0
        pend_tail = [None]
        wk3 = din["w_k"].rearrange("(kc p) d -> p kc d", p=128)
        wv3 = din["w_v"].rearrange("(kc p) d -> p kc d", p=128)
        for hp in range(8):
            i = load_wg_slot(wk3[:, :, hp * 256:(hp + 1) * 256], 256, (KC, 256))
            W3 = wgv(i).rearrange("p (a b) -> p a b", a=KC)
            for hh in range(2):
                head = hp * 2 + hh
                for tt, (c0, w) in enumerate(tiles):
                    k = st["gen"]
                    st["gen"] += 1
                    pb, pB = PS[4 + k % 4], ps_b[4 + k % 4]
                    for kc in range(KC):
                        pr.emit("pe", lambda e, pb=pb, W3=W3, kc=kc, hh=hh, c0=c0, w=w: e.matmul(
                            pb[:, 0:w], W3[:, kc, hh * 128:(hh + 1) * 128], Hap(kc, c0, w),
                            start=(kc == 0), stop=(kc == KC - 1)),
                            reads=[wg_b[i], H[kc][tt]], writes=[pB])
                    ki = cnt % NST
                    rsi = cnt % 2
                    cnt += 1
                    kst = view(OFF_KST + ki * 1024, 512, BF16)
                    rs = view(OFF_RS + rsi * 2048, 512, F32)
                    lo = max(tok_lo, c0)
                    hi = c0 + w
                    dcol = dst_lo + (lo - tok_lo)

                    def after(kst=kst, head=head, lo=lo, hi=hi, c0=c0, dcol=dcol, ki=ki):
                        if hi > lo:
                            pr.emit("sp", lambda e: e.dma_start(
                                out=KTd[head, :, dcol:dcol + (hi - lo)], in_=kst[:, lo - c0:hi - c0]),
                                reads=[kst_b[ki]], pwrites=[KTd_b[head]], dkey="kst%d" % ki)
                    tl_ = head_norm(pb, pB, w, 2, kst[:, 0:w], dict(writes=[kst_b[ki]]), (1.0 / DH, EPSC),
                                    rs[:, 0:w], rs_b[rsi], after=after)
                    if pend_tail[0] is not None:
                        pend_tail[0]()
                    pend_tail[0] = tl_
        if pend_tail[0] is not None:
            pend_tail[0]()
            pend_tail[0] = None
        nch = (T - tok_lo) // 128
        for vp in range(8):
            i = load_wg_slot(wv3[:, :, vp * 256:(vp + 1) * 256], 256, (KC, 256))
            W3 = wgv(i).rearrange("p (a b) -> p a b", a=KC)
            for ch in range(nch):
                t0 = tok_lo + ch * 128
                tl = tiles_overlapping(t0, 128)
                k = st["gen"]
                st["gen"] += 1
                pb, pB = PS[4 + k % 4], ps_b[4 + k % 4]
                for kc in range(KC):
                    pr.emit("pe", lambda e, pb=pb, W3=W3, kc=kc, t0=t0: e.matmul(
                        pb[:, 0:256], Hap(kc, t0, 128), W3[:, kc, :], start=(kc == 0), stop=(kc == KC - 1)),
                        reads=[wg_b[i]] + [H[kc][x] for x in tl], writes=[pB])
                vi = cnt % NST
                cnt += 1
                vst = view(OFF_VST + vi * 512, 256, BF16)
                pr.emit("act", lambda e, vst=vst, pb=pb: e.activation(out=vst, in_=pb[:, 0:256], func=AF.Copy),
                        reads=[pB], writes=[vst_b[vi]])
                r0 = dst_lo + ch * 128
                pr.emit("sp", lambda e, vst=vst, r0=r0, vp=vp: e.dma_start(
                    out=Vd[r0:r0 + 128, vp * 256:(vp + 1) * 256], in_=vst),
                    reads=[vst_b[vi]], pwrites=[Vd_b], dkey="vst%d" % vi)

    def attn(lb):
        l = 2 + lb
        X, H, tiles, T = st["X"], st["H"], st["tiles"], st["T"]
        assert T == T_OWN
        rmsnorm(V_MIX + l)
        o = OFF_R
        OFF_QT = o; o += 32768
        OFF_KT = o; o += 2 * 3072
        OFF_VH = o; o += 2 * 3072
        OFF_BI = o; o += 2 * 2560
        OFF_TMP = o; o += 3 * 2560
        OFF_PT = o; o += 3 * 1280
        OFF_RD = o; o += 2 * 512
        OFF_RS = o; o += 2 * 2048
        assert o <= TOT
        qt_b = [[pr.sbuf(OFF_QT + (h * 1024 + c0) * 2, w * 2, "qt") for (c0, w) in tiles] for h in range(NH)]
        kt_b = [pr.sbuf(OFF_KT + i * 3072, 3072, "kt") for i in range(2)]
        vh_b = [pr.sbuf(OFF_VH + i * 3072, 3072, "vh") for i in range(2)]
        bi_b = [pr.sbuf(OFF_BI + i * 2560, 2560, "bi") for i in range(2)]
        tmp_b = [pr.sbuf(OFF_TMP + i * 2560, 2560, "tmp") for i in range(3)]
        pt_b = [pr.sbuf(OFF_PT + i * 1280, 1280, "pt") for i in range(3)]
        rd_b = [pr.sbuf(OFF_RD + i * 512, 512, "rd") for i in range(2)]
        rs_b = [pr.sbuf(OFF_RS + i * 2048, 2048, "rs") for i in range(2)]

        def QTap(h, c0, w):
            return view(OFF_QT + (h * 1024 + c0) * 2, w, BF16)

        wq3 = din["w_q"][lb].rearrange("(kc p) d -> p kc d", p=128)
        cnt = 0
        pend_tail = [None]
        for hp in range(8):
            i = load_wg_slot(wq3[:, :, hp * 256:(hp + 1) * 256], 256, (KC, 256))
            W3 = wgv(i).rearrange("p (a b) -> p a b", a=KC)
            for hh in range(2):
                head = hp * 2 + hh
                for tt, (c0, w) in enumerate(tiles):
                    k = st["gen"]
                    st["gen"] += 1
                    pb, pB = PS[4 + k % 4], ps_b[4 + k % 4]
                    for kc in range(KC):
                        pr.emit("pe", lambda e, pb=pb, W3=W3, kc=kc, hh=hh, c0=c0, w=w: e.matmul(
                            pb[:, 0:w], W3[:, kc, hh * 128:(hh + 1) * 128], Hap(kc, c0, w),
                            start=(kc == 0), stop=(kc == KC - 1)),
                            reads=[wg_b[i], H[kc][tt]], writes=[pB])
                    ri = cnt % 2
                    cnt += 1
                    rs = view(OFF_RS + ri * 2048, 512, F32)
                    tl_ = head_norm(pb, pB, w, lb, QTap(head, c0, w), dict(writes=[qt_b[head][tt]]),
                                    (1.0, EPSQ), rs[:, 0:w], rs_b[ri])
                    if pend_tail[0] is not None:
                        pend_tail[0]()
                    pend_tail[0] = tl_
        if pend_tail[0] is not None:
            pend_tail[0]()
            pend_tail[0] = None

        def load_head(h):
            i = h % 2
            ktv = view(OFF_KT + i * 3072, NKEY, BF16)
            vhv = view(OFF_VH + i * 3072, NKEY, BF16).rearrange("p (a b) -> p a b", a=12)
            biv = view(OFF_BI + i * 2560, 640, F32)
            pr.emit("sp", lambda e: e.dma_start(out=ktv, in_=KTd[h]), reads=[KTd_b[h]], writes=[kt_b[i]],
                    dkey="kt%d" % i)
            pr.emit("sp", lambda e: e.dma_start(
                out=vhv, in_=Vd[:, h * 128:(h + 1) * 128].rearrange("(a p) d -> p a d", p=128)),
                reads=[Vd_b], writes=[vh_b[i]], dkey="vh%d" % i)
            pr.emit("sp", lambda e: e.dma_start(out=biv, in_=din["biasG"][lb, h]), writes=[bi_b[i]],
                    dkey="bi%d" % i)
            pr.emit("pool", lambda e: e.tensor_tensor(out=biv, in0=biv, in1=MASKG, op=ALU.add),
                    reads=[bi_b[i], const_b], writes=[bi_b[i]])

        sc = dict(n=0)

        def scores(h, qt):
            i = h % 2
            n = sc["n"]
            sc["n"] += 1
            sa, sA = PS[(n % 3) * 2], ps_b[(n % 3) * 2]
            sb, sB = PS[(n % 3) * 2 + 1], ps_b[(n % 3) * 2 + 1]
            ktv = view(OFF_KT + i * 3072, NKEY, BF16)
            biv = view(OFF_BI + i * 2560, 640, F32)
            tmpv = view(OFF_TMP + (n % 3) * 2560, 640, F32)
            ptv = view(OFF_PT + (n % 3) * 1280, 640, BF16)
            q0 = qt * 128
            tq = q0 // 512
            for kb in range(5):
                dst = sa[:, kb * 128:(kb + 1) * 128] if kb < 4 else sb[:, 0:128]
                dB = sA if kb < 4 else sB
                k0 = (qt + kb) * 128
                pr.emit("pe", lambda e, dst=dst, k0=k0, q0=q0: e.matmul(
                    dst, ktv[:, k0:k0 + 128], QTap(h, q0, 128), start=True, stop=True),
                    reads=[kt_b[i], qt_b[h][tq]], pwrites=[dB])
            nh_ = max(0, 4 - qt)
            segs = []
            if nh_ > 0:
                segs.append((0, min(nh_, 4), HMASK))
            if nh_ < 4:
                segs.append((nh_, 4, ZCOL))
            first = True
            for (a, b, colap) in segs:
                kw = dict(writes=[tmp_b[n % 3]]) if first else dict(pwrites=[tmp_b[n % 3]])
                first = False
                pr.emit("dve", lambda e, a=a, b=b, colap=colap: e.scalar_tensor_tensor(
                    out=tmpv[:, a * 128:b * 128], in0=sa[:, a * 128:b * 128], scalar=colap[:, 0:1],
                    in1=biv[:, a * 128:b * 128], op0=ALU.add, op1=ALU.add),
                    reads=[sA, bi_b[i], const_b], **kw)
            pr.emit("dve", lambda e: e.tensor_tensor(
                out=tmpv[:, 512:640], in0=sb[:, 0:128], in1=biv[:, 512:640], op=ALU.add),
                reads=[sB, bi_b[i]], pwrites=[tmp_b[n % 3]])
            pr.emit("act", lambda e: e.activation(out=ptv, in_=tmpv, func=AF.Exp),
                    reads=[tmp_b[n % 3]], writes=[pt_b[n % 3]])
            return n

        def pv(h, qt, n):
            i = h % 2
            odb, oB = PS[6 + n % 2], ps_b[6 + n % 2]
            vhv = view(OFF_VH + i * 3072, NKEY, BF16).rearrange("p (a b) -> p a b", a=12)
            ptv = view(OFF_PT + (n % 3) * 1280, 640, BF16)
            rdv = view(OFF_RD + (n % 2) * 512, 128, F32)
            q0 = qt * 128
            tq = q0 // 512
            for kb in range(5):
                kw = dict(writes=[oB]) if kb == 0 else dict(pwrites=[oB])
                pr.emit("pe", lambda e, kb=kb: e.matmul(
                    odb[:, 0:128], vhv[:, qt + kb, :], ptv[:, kb * 128:(kb + 1) * 128],
                    start=(kb == 0), stop=(kb == 4)),
                    reads=[vh_b[i], pt_b[n % 3]], **kw)
            for kb in range(5):
                pr.emit("pe", lambda e, kb=kb: e.matmul(
                    odb[:, 128:256], ONES, ptv[:, kb * 128:(kb + 1) * 128],
                    start=(kb == 0), stop=(kb == 4)),
                    reads=[const_b, pt_b[n % 3]], pwrites=[oB])
            pr.emit("act", lambda e: e.activation(out=rdv, in_=odb[:, 128:256], func=AF.Ln),
                    reads=[oB], writes=[rd_b[n % 2]])
            pr.emit("act", lambda e: e.activation(out=rdv, in_=rdv, func=AF.Exp, scale=-1.0),
                    reads=[rd_b[n % 2]], writes=[rd_b[n % 2]])

            def tail():
                pr.emit("dve", lambda e: e.tensor_tensor(
                    out=Hap(h, q0, 128), in0=odb[:, 0:128], in1=rdv, op=ALU.mult),
                    reads=[oB, rd_b[n % 2]], pwrites=[H[h][tq]])
            return tail

        load_head(0)
        hist = []
        tails = []
        for h in range(NH):
            for qt in range(8):
                n = scores(h, qt)
                hist.append((h, qt, n))
                if tails:
                    tails.pop(0)()
                if len(hist) > 2:
                    tails.append(pv(*hist.pop(0)))
                if qt == 1 and h + 1 < NH:
                    load_head(h + 1)
        while hist or tails:
            if tails:
                tails.pop(0)()
            if hist:
                tails.append(pv(*hist.pop(0)))

        wo3 = din["w_o"][lb].rearrange("(kc p) d -> p kc d", p=128)
        for dp in range(8):
            i = load_wg_slot(wo3[:, :, dp * 256:(dp + 1) * 256], 256, (KC, 256))
            W3 = wgv(i).rearrange("p (a b) -> p a b", a=KC)
            for dd in range(2):
                dc = dp * 2 + dd
                for tt, (c0, w) in enumerate(tiles):
                    k = st["gen"]
                    st["gen"] += 1
                    pb, pB = PS[4 + k % 4], ps_b[4 + k % 4]
                    for kc in range(KC):
                        pr.emit("pe", lambda e, pb=pb, W3=W3, kc=kc, dd=dd, c0=c0, w=w: e.matmul(
                            pb[:, 0:w], W3[:, kc, dd * 128:(dd + 1) * 128], Hap(kc, c0, w),
                            start=(kc == 0), stop=(kc == KC - 1)),
                            reads=[wg_b[i], H[kc][tt]], writes=[pB])
                    pr.emit("dve", lambda e, pb=pb, dc=dc, c0=c0, w=w: e.tensor_tensor(
                        out=Xap(dc, c0, w), in0=pb[:, 0:w], in1=Xap(dc, c0, w), op=ALU.add),
                        reads=[pB, X[dc][tt]], writes=[X[dc][tt]])

    setup()
    own_tiles = [(0, 512), (512, 512)]
    halo_tiles = [(0, 272), (272, 272)]
    for pi, stp in enumerate(plan):
        kind = stp[0]
        if kind == "load_own":
            load_x(din["xT"], T_OWN, own_tiles)
        elif kind == "load_halo":
            load_x(din["xhT"], T_HALO, halo_tiles)
        elif kind == "retile":
            set_pass(st["T"], list(stp[1]))
        elif kind == "ffn":
            ffn(stp[1], stp[2])
        elif kind == "pool":
            pool(stp[1], stp[2])
        elif kind == "kv_halo":
            nxt = plan[pi + 1][0] if pi + 1 < len(plan) else None
            if nxt == "load_own":
                kv(T_HALO - HALO_KV, 0,
                   after_norm=lambda: st.__setitem__("xpre", prefetch_x(din["xT"], T_OWN, own_tiles)))
            else:
                kv(T_HALO - HALO_KV, 0)
        elif kind == "kv_own":
            kv(0, HALO_KV)
        elif kind == "attn":
            attn(stp[1])
        elif kind == "store":
            store_x()
        else:
            raise ValueError(kind)
    pr.lower(nc)
    nc._in_names = list(din.keys())
    nc._n_ops = {e: len(pr.ops[e]) for e in ENGS}
    return nc


_TA = ((0, 272), (272, 272))
_TB = ((16, 264), (280, 264))
_TC = ((32, 256), (288, 256))
FULL_PLAN = (
    [("load_halo",), ("ffn", 0, 1), ("pool", 0, True), ("retile", _TB), ("ffn", 0, 2), ("ffn", 1, 1),
     ("retile", _TA), ("pool", 1, True), ("retile", _TC), ("ffn", 1, 2), ("kv_halo",), ("load_own",)]
    + [s for l in (0, 1) for s in (("ffn", l, 1), ("pool", l, False), ("ffn", l, 2))]
    + [("kv_own",)]
    + [s for lb in (0, 1) for s in (("ffn", 2 + lb, 1), ("attn", lb), ("ffn", 2 + lb, 2))]
    + [("store",)]
)


def _vec_layout(v):
    return np.ascontiguousarray(np.asarray(v, np.float32).reshape(KC, 128).T)


def host_prep(inputs):
    x = np.asarray(inputs["x"], np.float32)[0]
    vecs = np.zeros((128, NVEC * KC), np.float32)

    def put(idx, v):
        vecs[:, idx * KC:(idx + 1) * KC] = _vec_layout(v)
    for l in range(4):
        put(V_FFN1 + l, inputs["ffn1_norm"][l])
        put(V_MIX + l, inputs["mix_norm"][l])
        put(V_FFN2 + l, inputs["ffn2_norm"][l])
    put(V_KV, inputs["kv_norm"])
    for l in range(2):
        put(V_PSC + l, inputs["pool_scale"][l])
    hg = np.zeros((128, 4), np.float32)
    hg[:, 0] = np.asarray(inputs["q_gain"], np.float32)[0]
    hg[:, 1] = np.asarray(inputs["q_gain"], np.float32)[1]
    hg[:, 2] = np.asarray(inputs["k_gain"], np.float32)
    m = np.arange(128)[:, None, None]
    kb = np.arange(5)[None, :, None]
    r = np.arange(128)[None, None, :]
    kk = kb * 128 + m
    rel = (512 + r) - kk
    idx = np.clip(rel, -63, 128) + 63
    qc = 8 + r // 64
    kc_ = kk // 64
    valid = (qc - kc_ >= 0) & (qc - kc_ <= 8)
    maskG = np.where(valid, 0.0, NEG).astype(np.float32).reshape(128, 640)
    rb = np.asarray(inputs["rel_bias"], np.float32)
    biasG = np.ascontiguousarray(rb[:, :, idx].reshape(2, NH, 128, 640))
    xT = np.ascontiguousarray(x.T)
    shared = {"vecs": vecs, "hg": hg, "maskG": maskG, "biasG": biasG}
    for nm in ("ffn1_w_gate", "ffn1_w_up", "ffn1_w_down", "ffn2_w_gate", "ffn2_w_up", "ffn2_w_down",
               "pool_w", "w_k", "w_v", "w_q", "w_o"):
        shared[nm] = np.ascontiguousarray(np.asarray(inputs[nm], np.float32))
    maps = []
    for c in range(NCORES):
        t0 = c * T_OWN
        d = dict(shared)
        d["xT"] = np.ascontiguousarray(xT[:, t0:t0 + T_OWN])
        xh = np.zeros((D, T_HALO), np.float32)
        lo = t0 - T_HALO
        if lo >= 0:
            xh[:, :] = xT[:, lo:t0]
        elif t0 > 0:
            xh[:, -t0:] = xT[:, 0:t0]
        d["xhT"] = xh
        d["hmask"] = np.full((128, 1), 0.0 if c > 0 else NEG, np.float32)
        invc = np.zeros((128, 64), np.float32)
        for g, wdw in enumerate(POOL_W):
            for t in range(16):
                cntv = min(t0 + t + 1, wdw)
                invc[:, g * 16 + t] = 1.0 / cntv
        d["invc"] = invc
        maps.append(d)
    return maps


_NC_CACHE = {}


def kernel(**inputs):
    maps = host_prep(inputs)
    if "full" not in _NC_CACHE:
        _NC_CACHE["full"] = build(FULL_PLAN)
    nc = _NC_CACHE["full"]
    maps = [{k: m[k] for k in nc._in_names} for m in maps]
    res = run_bass_kernel_spmd(nc, maps, core_ids=list(range(NCORES)))
    outs = [np.asarray(r["outT"], np.float32) for r in res.results]
    full = np.concatenate(outs, axis=1)
    return np.ascontiguousarray(full.T)[None].astype(np.float32)
```
